# Optimizing a Trainium2 kernel written in Bass

```python
import jax, jax.numpy as jnp
from jax import lax
import numpy as np

D_MODEL = 2048
BATCH = 16
SEQ = 2048
DEPTH = 1

HEAD_DIM = 64
N_Q_HEADS = 16
N_KV_HEADS = 2
ATTN_WIDTH = N_Q_HEADS * HEAD_DIM
KV_WIDTH = N_KV_HEADS * HEAD_DIM
CONV_WIDTH = D_MODEL - ATTN_WIDTH
MIX_WIDTH = ATTN_WIDTH + CONV_WIDTH
IN_WIDTH = ATTN_WIDTH + 2 * KV_WIDTH + 2 * CONV_WIDTH
WINDOW = 128
BLOCK = 128
CONV_KERNEL = 31
ROPE_THETA = 10000.0
D_FF = -(-8 * D_MODEL // (3 * 256)) * 256
LN_EPS = 1e-5
DEEPNORM_ALPHA = (2 * DEPTH) ** 0.25
DEEPNORM_BETA = (8 * DEPTH) ** -0.25

kernel_name = "hymba_swa_sink_conformer_conv_deepnorm"


def layer_norm(x, g, b):
    xf = x.astype(jnp.float32)
    mu = jnp.mean(xf, axis=-1, keepdims=True)
    var = jnp.mean(jnp.square(xf - mu), axis=-1, keepdims=True)
    y = (xf - mu) * lax.rsqrt(var + LN_EPS)
    return (y * g.astype(jnp.float32) + b.astype(jnp.float32)).astype(x.dtype)


def rope(x, positions):
    half = HEAD_DIM // 2
    inv_freq = 1.0 / (ROPE_THETA ** (jnp.arange(half, dtype=jnp.float32) * 2.0 / HEAD_DIM))
    ang = positions.astype(jnp.float32)[:, :, None] * inv_freq
    cos = jnp.cos(ang)[:, :, None, :]
    sin = jnp.sin(ang)[:, :, None, :]
    xf = x.astype(jnp.float32)
    x1, x2 = xf[..., :half], xf[..., half:]
    out = jnp.concatenate([x1 * cos - x2 * sin, x2 * cos + x1 * sin], axis=-1)
    return out.astype(x.dtype)


def sliding_window_sink_attention(q, k, v, sinks):
    b, s, _, _ = q.shape
    nb = s // BLOCK
    g = N_Q_HEADS // N_KV_HEADS
    qb = q.reshape(b, nb, BLOCK, N_KV_HEADS, g, HEAD_DIM)
    kb = k.reshape(b, nb, BLOCK, N_KV_HEADS, HEAD_DIM)
    vb = v.reshape(b, nb, BLOCK, N_KV_HEADS, HEAD_DIM)
    pad = ((0, 0), (1, 0), (0, 0), (0, 0), (0, 0))
    kk = jnp.concatenate([jnp.pad(kb, pad)[:, :-1], kb], axis=2)
    vv = jnp.concatenate([jnp.pad(vb, pad)[:, :-1], vb], axis=2)
    scores = jnp.einsum('bnqkgd,bnskd->bnkgqs', qb, kk).astype(jnp.float32) * (HEAD_DIM ** -0.5)
    qi = jnp.arange(BLOCK)[:, None]
    si = jnp.arange(2 * BLOCK)[None, :]
    diff = qi + BLOCK - si
    band = (diff >= 0) & (diff < WINDOW)
    key_pos = jnp.arange(nb)[:, None, None] * BLOCK - BLOCK + si[None]
    mask = band[None] & (key_pos >= 0)
    scores = jnp.where(mask[None, :, None, None], scores, -jnp.inf)
    sink = sinks.astype(jnp.float32).reshape(N_KV_HEADS, g)[None, None, :, :, None, None]
    m = jnp.maximum(jnp.max(scores, axis=-1, keepdims=True), sink)
    p = jnp.exp(scores - m)
    denom = jnp.sum(p, axis=-1, keepdims=True) + jnp.exp(sink - m)
    probs = (p / denom).astype(v.dtype)
    out = jnp.einsum('bnkgqs,bnskd->bnqkgd', probs, vv)
    return out.reshape(b, s, ATTN_WIDTH)


def conformer_conv(a, gate, w_dw, b_dw, ln_g, ln_b, w_pw2, b_pw2):
    u = a * jax.nn.sigmoid(gate)
    u = lax.conv_general_dilated(u, w_dw.astype(u.dtype), window_strides=(1,),
                                 padding=[(CONV_KERNEL - 1, 0)],
                                 dimension_numbers=('NWC', 'WIO', 'NWC'),
                                 feature_group_count=CONV_WIDTH) + b_dw
    u = jax.nn.silu(layer_norm(u, ln_g, ln_b))
    return u @ w_pw2 + b_pw2


def setup_inputs(seed: int = 0) -> dict:
    key = jax.random.key(seed)
    ks = jax.random.split(key, 24)
    f32 = jnp.float32
    nrm = lambda k, shape, scale: jax.random.normal(k, shape, f32) * scale
    L = DEPTH
    x = jax.random.normal(ks[0], (BATCH, SEQ, D_MODEL), f32)
    offsets = jax.random.randint(ks[1], (BATCH, 1), 0, 4096, dtype=jnp.int32)
    positions = offsets + jnp.arange(SEQ, dtype=jnp.int32)[None, :]
    return {
        "x": x,
        "positions": positions,
        "w_in": nrm(ks[2], (L, D_MODEL, IN_WIDTH), D_MODEL ** -0.5),
        "b_in": nrm(ks[3], (L, IN_WIDTH), 0.02),
        "sinks": nrm(ks[4], (L, N_Q_HEADS), 0.5),
        "w_dw": nrm(ks[5], (L, CONV_KERNEL, 1, CONV_WIDTH), CONV_KERNEL ** -0.5),
        "b_dw": nrm(ks[6], (L, CONV_WIDTH), 0.02),
        "conv_ln_g": 1.0 + nrm(ks[7], (L, CONV_WIDTH), 0.02),
        "conv_ln_b": nrm(ks[8], (L, CONV_WIDTH), 0.02),
        "w_pw2": nrm(ks[9], (L, CONV_WIDTH, CONV_WIDTH), CONV_WIDTH ** -0.5),
        "b_pw2": nrm(ks[10], (L, CONV_WIDTH), 0.02),
        "w_out": nrm(ks[11], (L, MIX_WIDTH, D_MODEL), MIX_WIDTH ** -0.5 * DEEPNORM_BETA),
        "b_out": nrm(ks[12], (L, D_MODEL), 0.02),
        "ln1_g": 1.0 + nrm(ks[13], (L, D_MODEL), 0.02),
        "ln1_b": nrm(ks[14], (L, D_MODEL), 0.02),
        "w_gate": nrm(ks[15], (L, D_MODEL, D_FF), D_MODEL ** -0.5),
        "w_up": nrm(ks[16], (L, D_MODEL, D_FF), D_MODEL ** -0.5),
        "w_down": nrm(ks[17], (L, D_FF, D_MODEL), D_FF ** -0.5 * DEEPNORM_BETA),
        "ln2_g": 1.0 + nrm(ks[18], (L, D_MODEL), 0.02),
        "ln2_b": nrm(ks[19], (L, D_MODEL), 0.02),
    }


def reference(x, positions, w_in, b_in, sinks, w_dw, b_dw, conv_ln_g, conv_ln_b, w_pw2, b_pw2,
              w_out, b_out, ln1_g, ln1_b, w_gate, w_up, w_down, ln2_g, ln2_b):
    b, s, _ = x.shape
    o_k = ATTN_WIDTH
    o_v = o_k + KV_WIDTH
    o_a = o_v + KV_WIDTH
    o_g = o_a + CONV_WIDTH
    for l in range(DEPTH):
        h = x @ w_in[l] + b_in[l]
        q = rope(h[..., :o_k].reshape(b, s, N_Q_HEADS, HEAD_DIM), positions)
        k = rope(h[..., o_k:o_v].reshape(b, s, N_KV_HEADS, HEAD_DIM), positions)
        v = h[..., o_v:o_a].reshape(b, s, N_KV_HEADS, HEAD_DIM)
        attn = sliding_window_sink_attention(q, k, v, sinks[l])
        conv = conformer_conv(h[..., o_a:o_g], h[..., o_g:], w_dw[l], b_dw[l],
                              conv_ln_g[l], conv_ln_b[l], w_pw2[l], b_pw2[l])
        mix = jnp.concatenate([attn, conv], axis=-1) @ w_out[l] + b_out[l]
        x = layer_norm(DEEPNORM_ALPHA * x + mix, ln1_g[l], ln1_b[l])
        ffn = (jax.nn.silu(x @ w_gate[l]) * (x @ w_up[l])) @ w_down[l]
        x = layer_norm(DEEPNORM_ALPHA * x + ffn, ln2_g[l], ln2_b[l])
    return x
```

```python
import os
import numpy as np
from contextlib import ExitStack
import concourse.bass as bass
import concourse.mybir as mybir
from concourse.bass_utils import run_bass_kernel_spmd

F32 = mybir.dt.float32
BF16 = mybir.dt.bfloat16
I32 = mybir.dt.int32
AF = mybir.ActivationFunctionType
ALU = mybir.AluOpType
AX = mybir.AxisListType

NCORES = 8
D = 2048
SEQ = 2048
T = 512
NTILE_SEQ = SEQ // T
DFF = 5632
NF = DFF // 128
KW = 31
HALO = KW - 1
ALPHA = float(2.0 ** 0.25)
EPS = 1e-5
MAGIC = 12582912.0
TWO_PI = float(2 * np.pi)
C1 = 6.28125
C2 = float(2 * np.pi - 6.28125)
PI_LO = 3.1415925

PC_BIN = 0
PC_BDW = 25
PC_CLG = 33
PC_CLB = 41
PC_BPW = 49
PC_BOUT = 57
PC_G1 = 73
PC_B1 = 89
PC_G2 = 105
PC_B2 = 121
PC_INVF = 137
PC_SIGN = 138
PC_WDW = 139
NPC = PC_WDW + 8 * KW


class Slot:
    __slots__ = ("name", "w", "r", "conf", "excl")

    def __init__(self, name, excl=False):
        self.name = name
        self.excl = excl
        self.w = None
        self.r = {}
        self.conf = []


def alias(group_a, group_b):
    for a in group_a:
        for b in group_b:
            if b not in a.conf:
                a.conf.append(b)
            if a not in b.conf:
                b.conf.append(a)


class Sched:
    def __init__(self, nc, stack, same_engine_sync=True):
        self.nc = nc
        self.q = {k: [] for k in ("pe", "act", "dve", "pool", "sp")}
        self.cnt = {}
        self.semobj = {}
        for k in ("pe", "act", "dve", "pool"):
            self.semobj[("eng", k)] = stack.enter_context(nc.semaphore("sem_" + k))
            self.cnt[k] = 0
        self.observed = {k: {} for k in self.q}
        self.same = same_engine_sync
        self.stack = stack

    def dma_sem(self, name):
        key = ("dma", name)
        self.semobj[key] = self.stack.enter_context(self.nc.semaphore("dsem_" + name))
        self.cnt[key] = 0
        return key

    def op(self, eng, fn, reads=(), writes=(), dsem=None, ndma=1):
        deps = {}
        writes = list(writes) + [s for s in reads if s.excl]
        reads = [s for s in reads if not s.excl]

        def add(d):
            if d is not None and deps.get(d[0], 0) < d[1]:
                deps[d[0]] = d[1]

        for s in reads:
            add(s.w)
            for c in s.conf:
                add(c.w)
        for s in writes:
            add(s.w)
            for kv in s.r.items():
                add(kv)
            for c in s.conf:
                add(c.w)
                for kv in c.r.items():
                    add(kv)
        waits = []
        obs = self.observed[eng]
        for k, v in deps.items():
            if k == ("eng", eng) and not (self.same and eng != "pe"):
                continue
            if obs.get(k, 0) >= v:
                continue
            obs[k] = v
            waits.append((k, v))
        if dsem is not None:
            self.cnt[dsem] += 16 * ndma
            me = (dsem, self.cnt[dsem])
        else:
            self.cnt[eng] += 1
            me = (("eng", eng), self.cnt[eng])
        self.q[eng].append((waits, fn, me[0]))
        for s in reads:
            if s.r.get(me[0], 0) < me[1]:
                s.r[me[0]] = me[1]
        for s in writes:
            s.w = me
            s.r = {}
        return me

    def emit(self, final_waits):
        nc = self.nc
        with nc.Block() as block:
            def mk(engname):
                def body(e):
                    for waits, fn, key in self.q[engname]:
                        for k, v in waits:
                            e.wait_ge(self.semobj[k], v)
                        insts = fn(e)
                        if not isinstance(insts, (list, tuple)):
                            insts = [insts]
                        if key[0] == "dma":
                            for i in insts:
                                i.then_inc(self.semobj[key], 16)
                        else:
                            insts[-1].then_inc(self.semobj[key], 1)
                    if engname == "sp":
                        for k, v in final_waits:
                            e.wait_ge(self.semobj[k], v)
                return body
            block.tensor(mk("pe"))
            block.scalar(mk("act"))
            block.vector(mk("dve"))
            block.gpsimd(mk("pool"))
            block.sync(mk("sp"))


def build_nc(ntiles=2 * NTILE_SEQ, debug=False, upto=9):
    nc = bass.Bass("TRN2", target_bir_lowering=False)
    dram = nc.dram_tensor
    NTOK = 2 * SEQ
    x_d = dram("x", [NTOK, D], F32, kind="ExternalInput").ap()
    pos_d = dram("pos", [2, SEQ], I32, kind="ExternalInput").ap()
    win_d = dram("w_in", [D, 3328], F32, kind="ExternalInput").ap()
    wpw_d = dram("w_pw2", [1024, 1024], F32, kind="ExternalInput").ap()
    wout_d = dram("w_out", [D, D], F32, kind="ExternalInput").ap()
    wgu_d = dram("w_gu", [D, 2 * DFF], F32, kind="ExternalInput").ap()
    wdn_d = dram("w_down", [DFF, D], F32, kind="ExternalInput").ap()
    pcol_d = dram("pcol", [128, NPC], F32, kind="ExternalInput").ap()
    prow_d = dram("prow", [128, 144], F32, kind="ExternalInput").ap()
    cst_d = dram("cst", [128, 512], F32, kind="ExternalInput").ap()
    out_d = dram("out", [NTOK, D], F32, kind="ExternalOutput").ap()
    winb = dram("w_in_b", [D, 3328], BF16, kind="Internal").ap()
    wpwb = dram("w_pw2_b", [1024, 1024], BF16, kind="Internal").ap()
    woutb = dram("w_out_b", [D, D], BF16, kind="Internal").ap()
    wgub = dram("w_gu_b", [D, 2 * DFF], BF16, kind="Internal").ap()
    wdnb = dram("w_down_b", [DFF, D], BF16, kind="Internal").ap()
    dbg_out = {}
    if debug:
        for nm, shp in (("d_mix", [128, 16 * 512]), ("d_x1", [128, 16 * 512]), ("d_q", [128, 8 * 512]),
                        ("d_k", [128, 512]), ("d_u", [128, 8 * 542]), ("d_un", [128, 8 * 512])):
            dbg_out[nm] = dram(nm, shp, F32 if nm == "d_x1" else BF16, kind="ExternalOutput").ap()

    with ExitStack() as st:
        S = Sched(nc, st)
        sb = lambda name, shape, dt: st.enter_context(nc.sbuf_tensor("sb_" + name, shape, dt))
        R = sb("R", [128, 16, T], F32)
        xT = sb("xT", [128, 16, T], BF16)
        U = sb("U", [128, 24320], BF16)
        hT = U[:, 0:NF * T].rearrange("p (f t) -> p f t", t=T)
        P = U[:, 0:4096].rearrange("p (s k) -> p s k", k=256)
        PTv = U[:, 4096:8192].rearrange("p (a c q) -> p a c q", c=8, q=128)
        Dg = [U[:, 8192 + i * 3968: 8192 + (i + 1) * 3968].rearrange("p (j m) -> p j m", m=128) for i in range(2)]
        mixT = U[:, 16128:16128 + 8192].rearrange("p (c t) -> p c t", t=T)
        wbuf = [sb("wbuf%d" % i, [128, 16, 512], BF16) for i in range(2)]
        QX = sb("QX", [128, 2048], F32)
        UX = sb("UX", [128, 2048], F32)
        qT = QX[:].bitcast(BF16)[:, 0:8 * T].rearrange("p (c t) -> p c t", t=T)
        un = UX[:].bitcast(BF16)[:, 0:8 * T].rearrange("p (c t) -> p c t", t=T)
        xs = [QX[:], UX[:], U[:, 0:4096].bitcast(F32), U[:, 4096:8192].bitcast(F32)]
        YY = sb("YY", [128, 8, T], F32)
        yo = [YY[:, 0:4, :].rearrange("p a t -> p (a t)"), YY[:, 4:8, :].rearrange("p a t -> p (a t)")]
        tmp = [sb("tmp%d" % i, [128, T], F32) for i in range(4)]
        tmpb = [t[:].bitcast(BF16) for t in tmp]
        qsb = [sb("qsb%d" % i, [128, T], BF16) for i in range(2)]
        kT = sb("kT", [128, 128 + T], BF16)
        Vp = sb("Vp", [128, 5, 2, 128], BF16)
        cosT = sb("cosT", [128, T], F32)
        sinT = sb("sinT", [128, T], F32)
        uT = sb("uT", [128, 8, HALO + T], BF16)
        stm = sb("stm", [128, T], F32)
        stv = sb("stv", [128, T], F32)
        ident32 = sb("ident32", [128, 128], F32)
        onesb = sb("onesb", [128, 128], BF16)
        identb = sb("identb", [128, 128], BF16)
        rpermb = sb("rpermb", [128, 128], BF16)
        maskb = sb("maskb", [128, 256], BF16)
        negmb = sb("negmb", [128, 512], BF16)
        pcol = sb("pcol", [128, NPC], F32)
        prow = sb("prow", [128, 144], F32)
        ag = sb("ag", [128, 32], F32)
        sm = sb("sm", [128, 6, 16], F32)
        PS = st.enter_context(nc.psum_tensor("PS", [128, 8, 512], F32))

        mk = lambda n, k: [Slot("%s%d" % (n, i)) for i in range(k)]
        s_R = mk("R", 16); s_xT = mk("xT", 16); s_h = mk("h", NF); s_P = mk("P", 4); s_PT = mk("PT", 4)
        s_D = mk("D", 2); s_mix = mk("mix", 16); s_w = mk("w", 2); s_q = mk("q", 8); s_un = mk("un", 8)
        s_xs = mk("xs", 4); s_Y = mk("Y", 8); s_yo = mk("yo", 2); s_tmp = mk("tmp", 4); s_qsb = mk("qsb", 2)
        s_kT = Slot("kT"); s_V = Slot("V"); s_cos = Slot("cos"); s_sin = Slot("sin"); s_u = mk("u", 8)
        s_uh = Slot("uhalo"); s_stm = Slot("stm"); s_stv = Slot("stv"); s_c = Slot("const"); s_sm = mk("sm", 4); s_B = [Slot("B%d" % i, excl=True) for i in range(8)]
        s_cast = {}
        alias(s_h, s_P + s_PT + s_D + s_mix)
        alias([s_xs[0]], s_q)
        alias([s_xs[1]], s_un)
        alias(s_xs[2:4], s_h + s_P + s_PT)
        alias([s_yo[0]], s_Y[0:4])
        alias([s_yo[1]], s_Y[4:8])

        def cast(name, dst, src):
            s_cast[name] = Slot("cast_" + name)
            if os.environ.get("K_NOCAST"):
                return
            ds = S.dma_sem("c_" + name)
            S.op("pool", lambda e: e.dma_start(out=dst, in_=src), writes=[s_cast[name]], dsem=ds)

        cast("in", winb.rearrange("r (a b) -> r a b", b=832), win_d.rearrange("r (a b) -> r a b", b=832))
        d_cst = S.dma_sem("cst")
        ld_c = [None]

        def load_consts(e):
            return [e.dma_start(out=pcol[:], in_=pcol_d), e.dma_start(out=prow[:], in_=prow_d),
                    e.dma_start(out=tmp[0][:], in_=cst_d)]
        S.op("sp", load_consts, writes=[s_c, s_tmp[0]], dsem=d_cst, ndma=3)

        def mkconst(e):
            e.tensor_copy(out=ident32[:], in_=tmp[0][:, 0:128])
            e.tensor_copy(out=identb[:], in_=tmp[0][:, 0:128])
            e.tensor_copy(out=rpermb[:], in_=tmp[0][:, 128:256])
            e.tensor_copy(out=maskb[:], in_=tmp[0][:, 256:512])
            e.tensor_scalar(out=negmb[:, 0:256], in0=tmp[0][:, 256:512], scalar1=30000.0, scalar2=-30000.0, op0=ALU.mult, op1=ALU.add)
            e.tensor_scalar(out=negmb[:, 256:512], in0=tmp[0][:, 256:512], scalar1=30000.0, scalar2=-30000.0, op0=ALU.mult, op1=ALU.add)
            e.memset(onesb[:], 1.0)
            e.memset(Vp[:], 0.0)
            e.tensor_scalar(out=ag[:, 0:16], in0=pcol[:, PC_G1:PC_G1 + 16], scalar1=ALPHA, scalar2=None, op0=ALU.mult)
            return e.tensor_scalar(out=ag[:, 16:32], in0=pcol[:, PC_B1:PC_B1 + 16], scalar1=ALPHA, scalar2=None, op0=ALU.mult)
        S.op("dve", mkconst, reads=[s_tmp[0]], writes=[s_c, s_V])

        cast("pw", wpwb, wpw_d)
        cast("out", woutb, wout_d)
        GU_PIECES = [(0, 6), (6, 12), (12, 17), (17, 22)]
        for i, (a, b) in enumerate(GU_PIECES):
            cast("gu%d" % i, wgub[:, a * 512:b * 512].rearrange("r (a b) -> r a b", b=512),
                 wgu_d[:, a * 512:b * 512].rearrange("r (a b) -> r a b", b=512))
        for i in range(2):
            cast("dn%d" % i, wdnb[:, i * 1024:(i + 1) * 1024], wdn_d[:, i * 1024:(i + 1) * 1024])

        groups = []
        for _ in range(ntiles):
            for g in (6, 0, 1, 2, 3, 4, 5):
                nw = 512 if g < 6 else 256
                groups.append((winb[:, g * 512:g * 512 + nw].rearrange("(kc p) n -> p kc n", p=128), "in", 16, nw))
            for g in range(2):
                groups.append((wpwb[:, g * 512:(g + 1) * 512].rearrange("(kc p) n -> p kc n", p=128), "pw", 8, 512))
            for g in range(4):
                groups.append((woutb[:, g * 512:(g + 1) * 512].rearrange("(kc p) n -> p kc n", p=128), "out", 16, 512))
            for g in range(22):
                pi = [i for i, (a, b) in enumerate(GU_PIECES) if a <= g < b][0]
                groups.append((wgub[:, g * 512:(g + 1) * 512].rearrange("(kc p) n -> p kc n", p=128), "gu%d" % pi, 16, 512))
            for n in range(4):
                for kg in range(4):
                    groups.append((wdnb[kg * 1408:(kg + 1) * 1408, n * 512:(n + 1) * 512].rearrange("(kc p) n -> p kc n", p=128),
                                   "dn%d" % (n // 2), 11, 512))
        d_w = [S.dma_sem("w%d" % i) for i in range(2)]
        wstate = {"load": 0, "use": 0}

        def w_acquire():
            i = wstate["use"]
            while wstate["load"] < min(len(groups), i + 2):
                li = wstate["load"]
                src, cname, kc, nw = groups[li]
                bi = li % 2
                S.op("sp", lambda e, src=src, kc=kc, nw=nw, bi=bi: e.dma_start(out=wbuf[bi][:, 0:kc, 0:nw], in_=src),
                     reads=[s_cast[cname]], writes=[s_w[bi]], dsem=d_w[bi])
                wstate["load"] += 1
            wstate["use"] += 1
            return wbuf[i % 2], s_w[i % 2]

        d_xs = [S.dma_sem("xs%d" % i) for i in range(4)]
        d_yo = [S.dma_sem("yo%d" % i) for i in range(8)]
        d_pos = S.dma_sem("pos")
        dbg_sems = []

        def pcs(col):
            return pcol[:, col:col + 1]

        bank_rr = {"i": 0}

        def ln_finish(inv_n, b1=2, b2=3):
            S.op("dve", lambda e: e.tensor_scalar(out=stm[:], in0=PS[:, b1, :], scalar1=inv_n, scalar2=None, op0=ALU.mult),
                 reads=[s_B[b1]], writes=[s_stm])
            S.op("dve", lambda e: e.tensor_tensor(out=stv[:], in0=stm[:], in1=stm[:], op=ALU.mult), reads=[s_stm], writes=[s_stv])
            S.op("dve", lambda e: e.scalar_tensor_tensor(out=stv[:], in0=PS[:, b2, :], scalar=inv_n, in1=stv[:], op0=ALU.mult, op1=ALU.subtract),
                 reads=[s_B[b2]], writes=[s_stv])
            S.op("act", lambda e: e.activation(out=stv[:], in_=stv[:], func=AF.Sqrt, bias=EPS_AP[:, 0:1], scale=1.0),
                 reads=[s_c], writes=[s_stv])
            S.op("dve", lambda e: e.reciprocal(out=PS[:, b2, :], in_=stv[:]), reads=[s_stv], writes=[s_B[b2]])
            S.op("dve", lambda e: e.scalar_tensor_tensor(out=PS[:, b1, :], in0=stm[:], scalar=-1.0, in1=PS[:, b2, :], op0=ALU.mult, op1=ALU.mult),
                 reads=[s_stm, s_B[b2]], writes=[s_B[b1]])
            S.op("dve", lambda e: e.tensor_copy(out=stm[:], in_=PS[:, b1, :]), reads=[s_B[b1]], writes=[s_stm])

        def ln_normalize(ap, slot, idx, b1=2, b2=3):
            S.op("dve", lambda e: e.tensor_tensor(out=ap, in0=ap, in1=PS[:, b2, :], op=ALU.mult), reads=[s_B[b2]], writes=[slot])
            if idx % 2 == 0:
                S.op("dve", lambda e: e.tensor_tensor(out=ap, in0=ap, in1=PS[:, b1, :], op=ALU.add), reads=[s_B[b1]], writes=[slot])
            else:
                S.op("pool", lambda e: e.tensor_tensor(out=ap, in0=ap, in1=stm[:], op=ALU.add), reads=[s_stm], writes=[slot])

        EPS_AP = sb("epsap", [128, 1], F32)
        S.op("dve", lambda e: e.memset(EPS_AP[:], EPS), writes=[s_c])

        def stats_mm(src_ap, src_slot, bank, first, last):
            S.op("pe", lambda e: e.matmul(PS[:, bank, :], lhsT=onesb[:], rhs=src_ap, start=first, stop=last),
                 reads=[src_slot, s_c], writes=[s_B[bank]])

        def reduce_angle(a, sa, k, sk):
            S.op("dve", lambda e: e.tensor_scalar(out=k, in0=a, scalar1=float(1.0 / TWO_PI), scalar2=MAGIC, op0=ALU.mult, op1=ALU.add),
                 reads=[sa], writes=[sk])
            S.op("dve", lambda e: e.tensor_scalar(out=k, in0=k, scalar1=-MAGIC, scalar2=None, op0=ALU.add), writes=[sk])
            S.op("dve", lambda e: e.scalar_tensor_tensor(out=a, in0=k, scalar=-C1, in1=a, op0=ALU.mult, op1=ALU.add), reads=[sk], writes=[sa])
            S.op("dve", lambda e: e.scalar_tensor_tensor(out=a, in0=k, scalar=-C2, in1=a, op0=ALU.mult, op1=ALU.add), reads=[sk], writes=[sa])
            S.op("dve", lambda e: e.tensor_scalar(out=a, in0=a, scalar1=PI_LO, scalar2=-PI_LO, op0=ALU.min, op1=ALU.max), writes=[sa])

        def dbg_dump(name, ap, slots):
            if not debug:
                return
            ds = S.dma_sem("dbg_" + name)
            dbg_sems.append(ds)
            S.op("sp", lambda e: e.dma_start(out=dbg_out[name], in_=ap), reads=slots, writes=[], dsem=ds)

        def load_x_block(tok0, b, bi):
            S.op("sp", lambda e: e.dma_start(out=xs[bi], in_=x_d[tok0 + b * 128: tok0 + (b + 1) * 128, :]),
                 writes=[s_xs[bi]], dsem=d_xs[bi])

        def gen_tables(tix):
            sq_, tt_ = tix // NTILE_SEQ, tix % NTILE_SEQ
            S.op("sp", lambda e: e.dma_start(out=sinT[:].bitcast(I32),
                                             in_=pos_d[sq_:sq_ + 1, tt_ * T:(tt_ + 1) * T].partition_broadcast(128)),
                 writes=[s_sin], dsem=d_pos)
            S.op("dve", lambda e: e.tensor_copy(out=cosT[:], in_=sinT[:].bitcast(I32)), reads=[s_sin], writes=[s_cos])
            S.op("dve", lambda e: e.tensor_scalar(out=cosT[:], in0=cosT[:], scalar1=pcs(PC_INVF), scalar2=None, op0=ALU.mult),
                 reads=[s_c], writes=[s_cos])
            S.op("dve", lambda e: e.tensor_scalar(out=sinT[:], in0=cosT[:], scalar1=float(np.pi / 2), scalar2=None, op0=ALU.add),
                 reads=[s_cos], writes=[s_sin])
            reduce_angle(cosT[:], s_cos, tmp[2][:], s_tmp[2])
            reduce_angle(sinT[:], s_sin, tmp[2][:], s_tmp[2])
            S.op("act", lambda e: e.activation(out=tmp[3][:], in_=cosT[:], func=AF.Sin, scale=pcs(PC_SIGN)),
                 reads=[s_cos, s_c], writes=[s_tmp[3]])
            S.op("act", lambda e: e.activation(out=cosT[:], in_=sinT[:], func=AF.Sin), reads=[s_sin], writes=[s_cos])
            S.op("act", lambda e: e.activation(out=sinT[:], in_=tmp[3][:], func=AF.Identity), reads=[s_tmp[3]], writes=[s_sin])

        pend_stats = []

        def flush_stats():
            while pend_stats:
                pend_stats.pop(0)()

        def resid_stats(dc, bank, b1=2, b2=3, bias_col=None):
            if bias_col is None:
                S.op("dve", lambda e: e.tensor_tensor(out=R[:, dc, :], in0=PS[:, bank, :], in1=R[:, dc, :], op=ALU.add),
                     reads=[s_B[bank]], writes=[s_R[dc]])
            else:
                S.op("dve", lambda e: e.scalar_tensor_tensor(out=R[:, dc, :], in0=PS[:, bank, :], scalar=pcs(bias_col), in1=R[:, dc, :],
                                                             op0=ALU.add, op1=ALU.add), reads=[s_B[bank], s_c], writes=[s_R[dc]])
            sq = dc % 4
            while len(pend_stats) >= 4:
                pend_stats.pop(0)()

            def sqy(e):
                e.activation(out=tmpb[sq][:, 0:T], in_=R[:, dc, :], func=AF.Square)
                return e.activation(out=tmpb[sq][:, T:2 * T], in_=R[:, dc, :], func=AF.Identity)
            S.op("act", sqy, reads=[s_R[dc]], writes=[s_tmp[sq]])

            def later():
                stats_mm(tmpb[sq][:, T:2 * T], s_tmp[sq], b1, dc == 0, dc == 15)
                stats_mm(tmpb[sq][:, 0:T], s_tmp[sq], b2, dc == 0, dc == 15)
            pend_stats.append(later)

        def kouter_group(wb, wslot, src, src_slots, K, banks):
            for kc in range(K):
                def mm(e, kc=kc):
                    r = None
                    for cc in range(4):
                        r = e.matmul(PS[:, banks[cc], :], lhsT=wb[:, kc, cc * 128:(cc + 1) * 128], rhs=src[:, kc, :],
                                     start=(kc == 0), stop=(kc == K - 1))
                    return r
                S.op("pe", mm, reads=[wslot, src_slots[kc]], writes=[s_B[b] for b in banks])

        for ti in range(ntiles):
            seq = ti // NTILE_SEQ
            tt = ti % NTILE_SEQ
            tok0 = seq * SEQ + tt * T
            first = tt == 0
            dbg_here = debug and ti == 0

            if upto < -2:
                continue
            if ti == 0:
                gen_tables(0)

            if upto < -1:
                continue
            for b in range(4):
                bi = b
                if ti == 0:
                    load_x_block(tok0, b, bi)
                for g in range(4):
                    bank = g

                    def tr(e, g=g, bi=bi, bank=bank):
                        r = None
                        for i in range(4):
                            kc = 4 * g + i
                            r = e.transpose(out=PS[:, bank, i * 128:(i + 1) * 128], in_=xs[bi][:, kc * 128:(kc + 1) * 128],
                                            identity=ident32[:])
                        return r
                    S.op("pe", tr, reads=[s_xs[bi], s_c], writes=[s_B[bank]])

                    S.op("act", lambda e, g=g, b=b, bank=bank: e.activation(
                        out=R[:, 4 * g:4 * g + 4, b * 128:(b + 1) * 128], in_=PS[:, bank, :].rearrange("p (i t) -> p i t", t=128),
                        func=AF.Identity, scale=ALPHA), reads=[s_B[bank]], writes=s_R[4 * g:4 * g + 4])
                    S.op("dve", lambda e, g=g, b=b, bank=bank: e.tensor_copy(
                        out=xT[:, 4 * g:4 * g + 4, b * 128:(b + 1) * 128],
                        in_=PS[:, bank, :].rearrange("p (i t) -> p i t", t=128)),
                        reads=[s_B[bank]], writes=s_xT[4 * g:4 * g + 4])

            if upto < 0:
                continue
            if first:
                S.op("dve", lambda e: e.memset(uT[:, :, 0:HALO], 0.0), writes=[s_uh])
            acc_banks = [2, 3, 4]
            rot_banks = [5, 6]
            ctr = {"acc": 0, "rot": 0, "tmp": 0}

            def inproj_mm(wb, wslot, cc, bank):
                def mm(e, wb=wb, cc=cc, bank=bank):
                    r = None
                    for kc in range(16):
                        r = e.matmul(PS[:, bank, :], lhsT=wb[:, kc, cc * 128:(cc + 1) * 128], rhs=xT[:, kc, :],
                                     start=(kc == 0), stop=(kc == 15))
                    return r
                S.op("pe", mm, reads=[wslot] + s_xT, writes=[s_B[bank]])

            pend_rot = []

            def flush_rot():
                while pend_rot:
                    pend_rot.pop(0)()

            def qk_chunk(wb, wslot, cc, ch):
                bank = acc_banks[ctr["acc"] % 3]
                ctr["acc"] += 1
                inproj_mm(wb, wslot, cc, bank)
                qi = ctr["rot"] % 2
                rb = rot_banks[ctr["rot"] % 2]
                ctr["rot"] += 1
                t1 = 2 + (ctr["tmp"] % 2)
                ctr["tmp"] += 1
                t2 = 0 if t1 == 2 else 1
                flush_rot()
                S.op("act", lambda e: e.activation(out=qsb[qi][:], in_=PS[:, bank, :], func=AF.Identity, bias=pcs(PC_BIN + ch), scale=1.0),
                     reads=[s_B[bank], s_c], writes=[s_qsb[qi]])
                if ch < 8:
                    dst, dslot = qT[:, ch, :], s_q[ch]
                else:
                    dst, dslot = kT[:, 128:128 + T], s_kT

                def later():
                    S.op("pe", lambda e: e.matmul(PS[:, rb, :], lhsT=rpermb[:], rhs=qsb[qi][:], start=True, stop=True),
                         reads=[s_qsb[qi], s_c], writes=[s_B[rb]])
                    S.op("pool", lambda e: e.tensor_tensor(out=tmp[t1][:], in0=qsb[qi][:], in1=cosT[:], op=ALU.mult),
                         reads=[s_qsb[qi], s_cos], writes=[s_tmp[t1]])
                    S.op("dve", lambda e: e.tensor_tensor(out=tmp[t2][:], in0=PS[:, rb, :], in1=sinT[:], op=ALU.mult),
                         reads=[s_B[rb], s_sin], writes=[s_tmp[t2]])
                    S.op("dve", lambda e: e.tensor_tensor(out=dst, in0=tmp[t1][:], in1=tmp[t2][:], op=ALU.add),
                         reads=[s_tmp[t1], s_tmp[t2]], writes=[dslot])
                pend_rot.append(later)

            wb, wslot = w_acquire()
            qk_chunk(wb, wslot, 0, 24)
            for b in range(4):
                vb = 7 if b % 2 == 0 else 0

                def mmv(e, wb=wb, b=b, vb=vb):
                    r = None
                    for kc in range(16):
                        r = e.matmul(PS[:, vb, 0:128], lhsT=xT[:, kc, b * 128:(b + 1) * 128], rhs=wb[:, kc, 128:256],
                                     start=(kc == 0), stop=(kc == 15))
                    return r
                S.op("pe", mmv, reads=[wslot] + s_xT, writes=[s_B[vb]])
                flush_rot()

                def evv(e, b=b, vb=vb):
                    e.tensor_tensor(out=Vp[:, b + 1, 0, 0:64], in0=PS[:, vb, 0:64], in1=prow[:, 0:64], op=ALU.add)
                    return e.tensor_tensor(out=Vp[:, b + 1, 1, 64:128], in0=PS[:, vb, 64:128], in1=prow[:, 64:128], op=ALU.add)
                S.op("dve", evv, reads=[s_B[vb], s_c], writes=[s_V])
            for g in range(2):
                wb, wslot = w_acquire()
                for cc in range(4):
                    qk_chunk(wb, wslot, cc, 4 * g + cc)
            flush_rot()
            if dbg_here:
                dbg_dump("d_q", qT.rearrange("p c t -> p (c t)"), s_q)
                dbg_dump("d_k", kT[:, 128:128 + T], [s_kT])

            wstate2 = {}

            def pair_item(c):
                if c % 2 == 0:
                    wstate2["w"] = w_acquire()
                wb, wslot = wstate2["w"]
                cc0 = (c % 2) * 2
                sgi = c % 2
                inproj_mm(wb, wslot, cc0, 4)
                S.op("act", lambda e: e.activation(out=tmp[sgi][:], in_=PS[:, 4, :], func=AF.Sigmoid, bias=pcs(PC_BIN + 8 + 2 * c), scale=1.0),
                     reads=[s_B[4], s_c], writes=[s_tmp[sgi]])
                inproj_mm(wb, wslot, cc0 + 1, 5)
                S.op("dve", lambda e: e.scalar_tensor_tensor(out=uT[:, c, HALO:HALO + T], in0=PS[:, 5, :], scalar=pcs(PC_BIN + 9 + 2 * c),
                                                             in1=tmp[sgi][:], op0=ALU.add, op1=ALU.mult),
                     reads=[s_B[5], s_tmp[sgi], s_c], writes=[s_u[c]])

            def conv_prep(c):
                di = c % 2

                def dgen(e):
                    r = None
                    for jj in range(KW):
                        r = e.tensor_scalar(out=Dg[di][:, jj, :], in0=identb[:], scalar1=pcs(PC_WDW + c * KW + jj), scalar2=None, op0=ALU.mult)
                    return r
                S.op("dve", dgen, reads=[s_c], writes=[s_D[di]])

            def conv_item(c):
                di = c % 2
                bank = 4 + c % 2
                sq = 2 + c % 2

                def cv(e):
                    r = None
                    for jj in range(KW):
                        r = e.matmul(PS[:, bank, :], lhsT=Dg[di][:, jj, :], rhs=uT[:, c, jj:jj + T], start=(jj == 0), stop=(jj == KW - 1))
                    return r
                S.op("pe", cv, reads=[s_D[di], s_u[c], s_uh], writes=[s_B[bank]])
                S.op("act", lambda e: e.activation(out=YY[:, c, :], in_=PS[:, bank, :], func=AF.Identity, bias=pcs(PC_BDW + c), scale=1.0),
                     reads=[s_B[bank], s_c], writes=[s_Y[c]])

                def sqy(e):
                    e.activation(out=tmpb[sq][:, 0:T], in_=PS[:, bank, :], func=AF.Square, bias=pcs(PC_BDW + c), scale=1.0)
                    return e.activation(out=tmpb[sq][:, T:2 * T], in_=PS[:, bank, :], func=AF.Identity, bias=pcs(PC_BDW + c), scale=1.0)
                S.op("act", sqy, reads=[s_B[bank], s_c], writes=[s_tmp[sq]])
                def later():
                    stats_mm(tmpb[sq][:, T:2 * T], s_tmp[sq], 6, c == 0, c == 7)
                    stats_mm(tmpb[sq][:, 0:T], s_tmp[sq], 7, c == 0, c == 7)
                pend_stats.append(later)

            items = []
            for c in range(8):
                items.append(("pair", c))
                if c >= 1:
                    items.append(("conv", c - 1))
            items.append(("conv", 7))
            items.append(("cln", 0))
            for oc in range(8):
                items.append(("pw", oc))

            def cln_item():
                flush_stats()
                ln_finish(1.0 / 1024, 6, 7)
                for c in range(8):
                    ln_normalize(YY[:, c, :], s_Y[c], c, 6, 7)
                    S.op("act", lambda e, c=c: e.activation(out=un[:, c, :], in_=YY[:, c, :], func=AF.Silu, bias=pcs(PC_CLB + c),
                                                            scale=pcs(PC_CLG + c)),
                         reads=[s_Y[c], s_c], writes=[s_un[c]])

            def pw_item(oc):
                if oc % 4 == 0:
                    wstate2["pw"] = w_acquire()
                wb, wslot = wstate2["pw"]
                cc = oc % 4
                bank = 4 + oc % 2

                def mm(e):
                    r = None
                    for kc in range(8):
                        r = e.matmul(PS[:, bank, :], lhsT=wb[:, kc, cc * 128:(cc + 1) * 128], rhs=un[:, kc, :], start=(kc == 0), stop=(kc == 7))
                    return r
                S.op("pe", mm, reads=[wslot] + s_un, writes=[s_B[bank]])
                S.op("act", lambda e: e.activation(out=mixT[:, 8 + oc, :], in_=PS[:, bank, :], func=AF.Identity,
                                                   bias=pcs(PC_BPW + oc), scale=1.0),
                     reads=[s_B[bank], s_c], writes=[s_mix[8 + oc]])

            def run_item():
                if not items:
                    return
                kind, c = items.pop(0)
                if kind == "pair":
                    pair_item(c)
                    flush_stats()
                elif kind == "cln":
                    cln_item()
                elif kind == "pw":
                    pw_item(c)
                else:
                    todo = list(pend_stats)
                    del pend_stats[:]
                    conv_item(c)
                    for fn in todo:
                        fn()
                for k2, c2 in items[:2]:
                    if k2 == "conv" and c2 not in prepped:
                        conv_prep(c2)
                        prepped.add(c2)
                        break
            prepped = set()
            conv_prep(0)
            prepped.add(0)

            pend_T = []
            pend_V = []
            rmax, mmx, negm, Ex, sums, rden = [sm[:, i, :] for i in range(6)]
            sinkrow = prow[:, 128:144]
            for j in range(4):
                nb = tt * 4 + j
                k0 = 0 if nb > 0 else 128
                L = 256 - k0
                kb0 = k0 // 128
                for sg in range(4):
                    bA = 0
                    ptb = 2
                    psc = PS[:, bA:bA + 2, :].rearrange("p b (s k) -> p (b s) k", k=256)
                    sl = slice(4 * sg, 4 * sg + 4)

                    def sc(e, sg=sg, j=j, k0=k0, bA=bA):
                        r = None
                        for rr in range(2):
                            e.matmul(PS[:, bA + rr, :], lhsT=identb[:], rhs=negmb[:], start=True, stop=False)
                        for rr in range(2):
                            for ccx in range(2):
                                c = 2 * sg + ccx
                                col = ccx * 256
                                r = e.matmul(PS[:, bA + rr, col + k0:col + 256], lhsT=qT[rr * 64:(rr + 1) * 64, c, j * 128:(j + 1) * 128],
                                             rhs=kT[rr * 64:(rr + 1) * 64, j * 128 + k0:j * 128 + 256], start=False, stop=(ccx == 1))
                        return r
                    S.op("pe", sc, reads=[s_q[2 * sg], s_q[2 * sg + 1], s_kT, s_c], writes=[s_B[bA], s_B[bA + 1]])
                    while pend_T:
                        pend_T.pop(0)()

                    S.op("dve", lambda e, psc=psc, sl=sl, k0=k0: e.tensor_reduce(out=rmax[:, sl], in_=psc[:, :, k0:256], axis=AX.X, op=ALU.max),
                         reads=[s_B[bA], s_B[bA + 1]], writes=[s_sm[sg]])
                    S.op("dve", lambda e, sl=sl: e.scalar_tensor_tensor(out=mmx[:, sl], in0=rmax[:, sl], scalar=0.125, in1=sinkrow[:, sl],
                                                                          op0=ALU.mult, op1=ALU.max), reads=[s_c], writes=[s_sm[sg]])
                    S.op("dve", lambda e, sl=sl: e.tensor_scalar(out=negm[:, sl], in0=mmx[:, sl], scalar1=-1.0, scalar2=None, op0=ALU.mult),
                         writes=[s_sm[sg]])
                    S.op("dve", lambda e, sl=sl: e.tensor_tensor(out=Ex[:, sl], in0=negm[:, sl], in1=sinkrow[:, sl], op=ALU.add),
                         reads=[s_c], writes=[s_sm[sg]])

                    def ex(e, psc=psc, sg=sg, k0=k0, sl=sl):
                        for n in range(4):
                            s = 4 * sg + n
                            e.activation(out=P[:, s, k0:256], in_=psc[:, n, k0:256], func=AF.Exp, bias=negm[:, s:s + 1], scale=0.125,
                                         accum_out=sums[:, s:s + 1])
                        return e.activation(out=Ex[:, sl], in_=Ex[:, sl], func=AF.Exp)
                    S.op("act", ex, reads=[s_B[bA], s_B[bA + 1], s_sm[sg]], writes=[s_P[sg], s_sm[sg]])

                    S.op("dve", lambda e, sl=sl: e.tensor_tensor(out=sums[:, sl], in0=sums[:, sl], in1=Ex[:, sl], op=ALU.add), writes=[s_sm[sg]])
                    S.op("dve", lambda e, sl=sl: e.reciprocal(out=rden[:, sl], in_=sums[:, sl]), writes=[s_sm[sg]])
                    S.op("dve", lambda e, sl=sl, k0=k0, L=L: e.tensor_tensor(
                        out=P[:, sl, k0:256], in0=P[:, sl, k0:256],
                        in1=rden[:, sl].unsqueeze(2).broadcast_to([128, 4, L]), op=ALU.mult), reads=[s_sm[sg]], writes=[s_P[sg]])

                    if items and items[0][0] in ("pair", "conv"):
                        run_item()
                    while pend_V:
                        pend_V.pop(0)()

                    ptps = PS[:, ptb, :].bitcast(BF16)

                    def tpart(sg=sg, kb0=kb0, ptps=ptps, j=j):
                        def trp(e, sg=sg, kb0=kb0, ptps=ptps):
                            r = None
                            for kb in range(kb0, 2):
                                for n in range(4):
                                    idx = kb * 4 + n
                                    r = e.transpose(out=ptps[:, idx * 128:(idx + 1) * 128], in_=P[:, 4 * sg + n, kb * 128:(kb + 1) * 128],
                                                    identity=identb[:])
                            return r
                        S.op("pe", trp, reads=[s_P[sg], s_c], writes=[s_B[ptb]])
                        S.op("act", lambda e, sg=sg, kb0=kb0, ptps=ptps: e.activation(
                            out=PTv[:, 2 * kb0:4, 2 * sg:2 * sg + 2, :],
                            in_=ptps[:, kb0 * 512:1024].rearrange("p (a c q) -> p a c q", c=2, q=128), func=AF.Identity),
                            reads=[s_B[ptb]], writes=[s_PT[sg]])
                    def vpart(sg=sg, kb0=kb0, j=j):
                        if sg % 2 == 1:
                            cg = sg // 2
                            ob = 3

                            def pv(e, cg=cg, ob=ob, j=j, kb0=kb0):
                                r = None
                                n = 0
                                tot = (2 - kb0) * 2
                                for kb in range(kb0, 2):
                                    for rr in range(2):
                                        r = e.matmul(PS[:, ob, :], lhsT=Vp[:, j + kb, rr, :], rhs=PTv[:, kb * 2 + rr, 4 * cg:4 * cg + 4, :],
                                                     start=(n == 0), stop=(n == tot - 1))
                                        n += 1
                                return r
                            S.op("pe", pv, reads=[s_V, s_PT[sg - 1], s_PT[sg]], writes=[s_B[ob]])
                            S.op("dve", lambda e, cg=cg, ob=ob, j=j: e.tensor_copy(
                                out=mixT[:, 4 * cg:4 * cg + 4, j * 128:(j + 1) * 128], in_=PS[:, ob, :].rearrange("p (c q) -> p c q", q=128)),
                                reads=[s_B[ob]], writes=s_mix[4 * cg:4 * cg + 4])
                    pend_T.append(tpart)
                    pend_T.append(lambda vp=vpart: pend_V.append(vp))
            while pend_T:
                pend_T.pop(0)()
            while pend_V:
                pend_V.pop(0)()
            while items:
                run_item()
            flush_stats()
            if dbg_here:
                dbg_dump("d_u", uT[:].rearrange("p c t -> p (c t)"), s_u + [s_uh])

            if tt < NTILE_SEQ - 1:
                def carry(e):
                    e.tensor_copy(out=uT[:, :, 0:HALO], in_=uT[:, :, T:T + HALO])
                    e.tensor_copy(out=kT[:, 0:128], in_=kT[:, T:T + 128])
                    return e.tensor_copy(out=Vp[:, 0, :, :], in_=Vp[:, 4, :, :])
                S.op("dve", carry, reads=s_u, writes=[s_uh, s_kT, s_V])
            if dbg_here:
                dbg_dump("d_mix", mixT.rearrange("p c t -> p (c t)"), s_mix)

            if upto < 3:
                continue
            accb = [0, 1, 4]
            na = 0
            for g in range(4):
                wb, wslot = w_acquire()
                for cc in range(4):
                    dc = 4 * g + cc
                    bank = accb[na % 3]
                    na += 1

                    def mm(e, wb=wb, cc=cc, bank=bank):
                        r = None
                        for kc in range(16):
                            r = e.matmul(PS[:, bank, :], lhsT=wb[:, kc, cc * 128:(cc + 1) * 128], rhs=mixT[:, kc, :], start=(kc == 0), stop=(kc == 15))
                        return r
                    S.op("pe", mm, reads=[wslot] + s_mix, writes=[s_B[bank]])
                    flush_stats()
                    resid_stats(dc, bank, bias_col=PC_BOUT + dc)
            flush_stats()
            ln_finish(1.0 / D)
            for dc in range(16):
                ln_normalize(R[:, dc, :], s_R[dc], dc)
                S.op("act", lambda e, dc=dc: e.activation(out=xT[:, dc, :], in_=R[:, dc, :], func=AF.Identity, bias=pcs(PC_B1 + dc),
                                                          scale=pcs(PC_G1 + dc)), reads=[s_c, s_R[dc]], writes=[s_xT[dc]])
                S.op("act", lambda e, dc=dc: e.activation(out=R[:, dc, :], in_=R[:, dc, :], func=AF.Identity, bias=ag[:, 16 + dc:17 + dc],
                                                          scale=ag[:, dc:dc + 1]), reads=[s_c], writes=[s_R[dc]])
            if dbg_here:
                dbg_dump("d_x1", R[:].rearrange("p c t -> p (c t)"), s_R)

            if upto < 4:
                continue
            accb = [0, 1, 4, 5]
            na = 0
            for g in range(22):
                wb, wslot = w_acquire()
                if g == 4 and ti + 1 < ntiles:
                    gen_tables(ti + 1)
                if g == 0:
                    kouter_group(wb, wslot, xT, s_xT, 16, accb)
                for cc in range(4):
                    f = 2 * g + cc // 2
                    bank = accb[na % 4]
                    na += 1

                    def mm(e, wb=wb, cc=cc, bank=bank):
                        r = None
                        for kc in range(16):
                            r = e.matmul(PS[:, bank, :], lhsT=wb[:, kc, cc * 128:(cc + 1) * 128], rhs=xT[:, kc, :], start=(kc == 0), stop=(kc == 15))
                        return r
                    if g > 0:
                        S.op("pe", mm, reads=[wslot] + s_xT, writes=[s_B[bank]])
                    sgi = f % 2
                    if cc % 2 == 0:
                        S.op("act", lambda e, bank=bank, sgi=sgi: e.activation(out=tmp[sgi][:], in_=PS[:, bank, :], func=AF.Silu),
                             reads=[s_B[bank]], writes=[s_tmp[sgi]])
                    else:
                        S.op("dve", lambda e, bank=bank, sgi=sgi, f=f: e.tensor_tensor(out=hT[:, f, :], in0=PS[:, bank, :], in1=tmp[sgi][:], op=ALU.mult),
                             reads=[s_B[bank], s_tmp[sgi]], writes=[s_h[f]])

            if upto < 5:
                continue
            if ti + 1 < ntiles:
                nseq = (ti + 1) // NTILE_SEQ
                ntok0 = nseq * SEQ + ((ti + 1) % NTILE_SEQ) * T
                for b in range(2):
                    load_x_block(ntok0, b, b)
            accb = [0, 1, 4, 5]
            for n in range(4):
                for kg in range(4):
                    wb, wslot = w_acquire()
                    for cc in range(4):
                        bank = accb[cc]

                        def mm(e, wb=wb, cc=cc, bank=bank, kg=kg):
                            r = None
                            for k in range(11):
                                r = e.matmul(PS[:, bank, :], lhsT=wb[:, k, cc * 128:(cc + 1) * 128], rhs=hT[:, kg * 11 + k, :],
                                             start=(kg == 0 and k == 0), stop=(kg == 3 and k == 10))
                            return r
                        S.op("pe", mm, reads=[wslot] + s_h[kg * 11:(kg + 1) * 11], writes=[s_B[bank]])
                    if kg == 0:
                        flush_stats()
                for cc in range(4):
                    resid_stats(4 * n + cc, accb[cc])
            flush_stats()
            if ti + 1 < ntiles:
                for b in range(2, 4):
                    load_x_block(ntok0, b, b)
            ln_finish(1.0 / D)

            def norm_group(g):
                for dc in range(4 * g, 4 * g + 4):
                    ln_normalize(R[:, dc, :], s_R[dc], dc)
                    S.op("act", lambda e, dc=dc: e.activation(out=R[:, dc, :], in_=R[:, dc, :], func=AF.Identity,
                                                              bias=pcs(PC_B2 + dc), scale=pcs(PC_G2 + dc)),
                         reads=[s_c], writes=[s_R[dc]])
            norm_group(0)
            for g in range(4):
                if g + 1 < 4:
                    norm_group(g + 1)
                for b in range(4):
                    k = 4 * g + b
                    bank = 6 + k % 2
                    pi = k % 8

                    def trf(e, g=g, b=b, bank=bank):
                        r = None
                        for i in range(4):
                            dc = 4 * g + i
                            r = e.transpose(out=PS[:, bank, i * 128:(i + 1) * 128], in_=R[:, dc, b * 128:(b + 1) * 128], identity=ident32[:])
                        return r
                    S.op("pe", trf, reads=s_R[4 * g:4 * g + 4] + [s_c], writes=[s_B[bank]])
                    if k % 2 == 0:
                        S.op("act", lambda e, pi=pi, bank=bank: e.activation(out=YY[:, pi, :], in_=PS[:, bank, :], func=AF.Identity),
                             reads=[s_B[bank]], writes=[s_Y[pi]])
                    else:
                        S.op("dve", lambda e, pi=pi, bank=bank: e.tensor_copy(out=YY[:, pi, :], in_=PS[:, bank, :]),
                             reads=[s_B[bank]], writes=[s_Y[pi]])
                    S.op("pool", lambda e, b=b, g=g, pi=pi, tok0=tok0: e.dma_start(
                        out=out_d[tok0 + b * 128: tok0 + (b + 1) * 128, g * 512:(g + 1) * 512], in_=YY[:, pi, :]),
                        reads=[s_Y[pi]], writes=[], dsem=d_yo[pi])

        fw = [(d, S.cnt[d]) for d in d_yo]
        for ds in dbg_sems:
            fw.append((ds, S.cnt[ds]))
        S.emit(fw)
    return nc


def _prep_shared(inp):
    f32 = np.float32
    qcols = []
    for c in range(8):
        qcols += list(range(c * 64, c * 64 + 64)) + list(range((8 + c) * 64, (8 + c) * 64 + 64))
    kcols = list(range(1024, 1152))
    vcols = list(range(1152, 1280))
    a0, g0 = 1280, 2304
    conv = []
    for c in range(8):
        conv += list(range(g0 + c * 128, g0 + (c + 1) * 128)) + list(range(a0 + c * 128, a0 + (c + 1) * 128))
    perm = np.array(qcols + conv + kcols + vcols)
    w_in = np.ascontiguousarray(np.asarray(inp["w_in"])[0][:, perm], dtype=f32)
    b_in = np.asarray(inp["b_in"])[0][perm].astype(f32)
    rows = np.array(qcols + list(range(1024, 2048)))
    w_out = np.ascontiguousarray(np.asarray(inp["w_out"])[0][rows, :], dtype=f32)
    wg = np.asarray(inp["w_gate"])[0].reshape(D, NF, 1, 128)
    wu = np.asarray(inp["w_up"])[0].reshape(D, NF, 1, 128)
    w_gu = np.ascontiguousarray(np.concatenate([wg, wu], axis=2).reshape(D, 2 * DFF), dtype=f32)
    w_down = np.ascontiguousarray(np.asarray(inp["w_down"])[0], dtype=f32)
    w_pw2 = np.ascontiguousarray(np.asarray(inp["w_pw2"])[0], dtype=f32)

    pcol = np.zeros((128, NPC), f32)
    pcol[:, PC_BIN:PC_BIN + 25] = b_in[:3200].reshape(25, 128).T
    col = lambda v, n: np.asarray(v)[0].astype(f32).reshape(n, 128).T
    pcol[:, PC_BDW:PC_BDW + 8] = col(inp["b_dw"], 8)
    pcol[:, PC_CLG:PC_CLG + 8] = col(inp["conv_ln_g"], 8)
    pcol[:, PC_CLB:PC_CLB + 8] = col(inp["conv_ln_b"], 8)
    pcol[:, PC_BPW:PC_BPW + 8] = col(inp["b_pw2"], 8)
    pcol[:, PC_BOUT:PC_BOUT + 16] = col(inp["b_out"], 16)
    pcol[:, PC_G1:PC_G1 + 16] = col(inp["ln1_g"], 16)
    pcol[:, PC_B1:PC_B1 + 16] = col(inp["ln1_b"], 16)
    pcol[:, PC_G2:PC_G2 + 16] = col(inp["ln2_g"], 16)
    pcol[:, PC_B2:PC_B2 + 16] = col(inp["ln2_b"], 16)
    half = 32
    inv_freq = (1.0 / (np.float32(10000.0) ** (np.arange(half, dtype=f32) * np.float32(2.0) / np.float32(64)))).astype(f32)
    p = np.arange(128)
    pcol[:, PC_INVF] = inv_freq[p % 32]
    pcol[:, PC_SIGN] = np.where((p % 64) < 32, -1.0, 1.0)
    wdw = np.asarray(inp["w_dw"])[0][:, 0, :].astype(f32)
    pcol[:, PC_WDW:] = wdw.reshape(KW, 8, 128).transpose(2, 1, 0).reshape(128, 8 * KW)

    prow = np.zeros((128, 144), f32)
    prow[:, 0:128] = b_in[3200:3328][None, :]
    sinks = np.asarray(inp["sinks"])[0].astype(f32)
    slot = np.array([sinks[(2 * (s // 4) + (s % 2)) + 8 * ((s % 4) // 2)] for s in range(16)], f32)
    prow[:, 128:144] = slot[None, :]

    cst = np.zeros((128, 512), f32)
    cst[:, 0:128] = np.eye(128, dtype=f32)
    m = np.arange(128)
    partner = np.where((m % 64) < 32, m + 32, m - 32)
    cst[partner, 128 + m] = 1.0
    qi = np.arange(128)[:, None]
    kj = np.arange(128)[None, :]
    cst[:, 256:384] = (kj > qi).astype(f32)
    cst[:, 384:512] = (kj <= qi).astype(f32)
    return dict(w_in=w_in, w_pw2=w_pw2, w_out=w_out, w_gu=w_gu, w_down=w_down, pcol=pcol, prow=prow, cst=cst)


def kernel(**inputs):
    shared = _prep_shared(inputs)
    x = np.asarray(inputs["x"], dtype=np.float32)
    pos = np.asarray(inputs["positions"]).astype(np.int32)
    nc = build_nc()
    in_maps = []
    for c in range(NCORES):
        m = dict(shared)
        m["x"] = np.ascontiguousarray(x[2 * c:2 * c + 2].reshape(2 * SEQ, D))
        m["pos"] = np.ascontiguousarray(pos[2 * c:2 * c + 2])
        in_maps.append(m)
    res = run_bass_kernel_spmd(nc, in_maps, core_ids=list(range(NCORES)))
    out = np.concatenate([np.asarray(r["out"]).reshape(2, SEQ, D) for r in res.results], axis=0)
    return out.astype(np.float32)
```

```python
import os
import numpy as np
from contextlib import ExitStack
import concourse.bass as bass
import concourse.mybir as mybir
from concourse.bass_utils import run_bass_kernel_spmd

F32 = mybir.dt.float32
BF16 = mybir.dt.bfloat16
I32 = mybir.dt.int32
AF = mybir.ActivationFunctionType
ALU = mybir.AluOpType
AX = mybir.AxisListType

NCORES = 8
D = 2048
SEQ = 2048
T = 512
NTILE_SEQ = SEQ // T
DFF = 5632
NF = DFF // 128
KW = 31
HALO = KW - 1
ALPHA = float(2.0 ** 0.25)
EPS = 1e-5
MAGIC = 12582912.0
TWO_PI = float(2 * np.pi)
C1 = 6.28125
C2 = float(2 * np.pi - 6.28125)
PI_LO = 3.1415925

PC_BIN = 0
PC_BDW = 25
PC_CLG = 33
PC_CLB = 41
PC_BPW = 49
PC_BOUT = 57
PC_G1 = 73
PC_B1 = 89
PC_G2 = 105
PC_B2 = 121
PC_INVF = 137
PC_SIGN = 138
PC_WDW = 139
NPC = PC_WDW + 8 * KW


class Slot:
    __slots__ = ("name", "w", "r", "conf", "excl")

    def __init__(self, name, excl=False):
        self.name = name
        self.excl = excl
        self.w = None
        self.r = {}
        self.conf = []


def alias(group_a, group_b):
    for a in group_a:
        for b in group_b:
            if b not in a.conf:
                a.conf.append(b)
            if a not in b.conf:
                b.conf.append(a)


class Sched:
    def __init__(self, nc, stack, same_engine_sync=True):
        self.nc = nc
        self.q = {k: [] for k in ("pe", "act", "dve", "pool", "sp")}
        self.cnt = {}
        self.semobj = {}
        for k in ("pe", "act", "dve", "pool"):
            self.semobj[("eng", k)] = stack.enter_context(nc.semaphore("sem_" + k))
            self.cnt[k] = 0
        self.observed = {k: {} for k in self.q}
        self.same = same_engine_sync
        self.stack = stack

    def dma_sem(self, name):
        key = ("dma", name)
        self.semobj[key] = self.stack.enter_context(self.nc.semaphore("dsem_" + name))
        self.cnt[key] = 0
        return key

    def op(self, eng, fn, reads=(), writes=(), dsem=None, ndma=1):
        deps = {}
        writes = list(writes) + [s for s in reads if s.excl]
        reads = [s for s in reads if not s.excl]

        def add(d):
            if d is not None and deps.get(d[0], 0) < d[1]:
                deps[d[0]] = d[1]

        for s in reads:
            add(s.w)
            for c in s.conf:
                add(c.w)
        for s in writes:
            add(s.w)
            for kv in s.r.items():
                add(kv)
            for c in s.conf:
                add(c.w)
                for kv in c.r.items():
                    add(kv)
        waits = []
        obs = self.observed[eng]
        for k, v in deps.items():
            if k == ("eng", eng) and not (self.same and eng != "pe"):
                continue
            if obs.get(k, 0) >= v:
                continue
            obs[k] = v
            waits.append((k, v))
        if dsem is not None:
            self.cnt[dsem] += 16 * ndma
            me = (dsem, self.cnt[dsem])
        else:
            self.cnt[eng] += 1
            me = (("eng", eng), self.cnt[eng])
        self.q[eng].append((waits, fn, me[0]))
        for s in reads:
            if s.r.get(me[0], 0) < me[1]:
                s.r[me[0]] = me[1]
        for s in writes:
            s.w = me
            s.r = {}
        return me

    def emit(self, final_waits):
        nc = self.nc
        with nc.Block() as block:
            def mk(engname):
                def body(e):
                    for waits, fn, key in self.q[engname]:
                        for k, v in waits:
                            e.wait_ge(self.semobj[k], v)
                        insts = fn(e)
                        if not isinstance(insts, (list, tuple)):
                            insts = [insts]
                        if key[0] == "dma":
                            for i in insts:
                                i.then_inc(self.semobj[key], 16)
                        else:
                            insts[-1].then_inc(self.semobj[key], 1)
                    if engname == "sp":
                        for k, v in final_waits:
                            e.wait_ge(self.semobj[k], v)
                return body
            block.tensor(mk("pe"))
            block.scalar(mk("act"))
            block.vector(mk("dve"))
            block.gpsimd(mk("pool"))
            block.sync(mk("sp"))


def build_nc(ntiles=2 * NTILE_SEQ, debug=False, upto=9):
    nc = bass.Bass("TRN2", target_bir_lowering=False)
    dram = nc.dram_tensor
    NTOK = 2 * SEQ
    x_d = dram("x", [NTOK, D], F32, kind="ExternalInput").ap()
    pos_d = dram("pos", [2, SEQ], I32, kind="ExternalInput").ap()
    win_d = dram("w_in", [D, 3328], F32, kind="ExternalInput").ap()
    wpw_d = dram("w_pw2", [1024, 1024], F32, kind="ExternalInput").ap()
    wout_d = dram("w_out", [D, D], F32, kind="ExternalInput").ap()
    wgu_d = dram("w_gu", [D, 2 * DFF], F32, kind="ExternalInput").ap()
    wdn_d = dram("w_down", [DFF, D], F32, kind="ExternalInput").ap()
    pcol_d = dram("pcol", [128, NPC], F32, kind="ExternalInput").ap()
    prow_d = dram("prow", [128, 144], F32, kind="ExternalInput").ap()
    cst_d = dram("cst", [128, 512], F32, kind="ExternalInput").ap()
    out_d = dram("out", [NTOK, D], F32, kind="ExternalOutput").ap()
    winb = dram("w_in_b", [D, 3328], BF16, kind="Internal").ap()
    wpwb = dram("w_pw2_b", [1024, 1024], BF16, kind="Internal").ap()
    woutb = dram("w_out_b", [D, D], BF16, kind="Internal").ap()
    wgub = dram("w_gu_b", [D, 2 * DFF], BF16, kind="Internal").ap()
    wdnb = dram("w_down_b", [DFF, D], BF16, kind="Internal").ap()
    dbg_out = {}
    if debug:
        for nm, shp in (("d_mix", [128, 16 * 512]), ("d_x1", [128, 16 * 512]), ("d_q", [128, 8 * 512]),
                        ("d_k", [128, 512]), ("d_u", [128, 8 * 542]), ("d_un", [128, 8 * 512])):
            dbg_out[nm] = dram(nm, shp, F32 if nm == "d_x1" else BF16, kind="ExternalOutput").ap()

    with ExitStack() as st:
        S = Sched(nc, st)
        sb = lambda name, shape, dt: st.enter_context(nc.sbuf_tensor("sb_" + name, shape, dt))
        R = sb("R", [128, 16, T], F32)
        xT = sb("xT", [128, 16, T], BF16)
        U = sb("U", [128, 24320], BF16)
        hT = U[:, 0:NF * T].rearrange("p (f t) -> p f t", t=T)
        P = U[:, 0:4096].rearrange("p (s k) -> p s k", k=256)
        PTv = U[:, 4096:8192].rearrange("p (a c q) -> p a c q", c=8, q=128)
        Dg = [U[:, 8192 + i * 3968: 8192 + (i + 1) * 3968].rearrange("p (j m) -> p j m", m=128) for i in range(2)]
        mixT = U[:, 16128:16128 + 8192].rearrange("p (c t) -> p c t", t=T)
        wbuf = [sb("wbuf%d" % i, [128, 16, 512], BF16) for i in range(2)]
        QX = sb("QX", [128, 2048], F32)
        UX = sb("UX", [128, 2048], F32)
        qT = QX[:].bitcast(BF16)[:, 0:8 * T].rearrange("p (c t) -> p c t", t=T)
        un = UX[:].bitcast(BF16)[:, 0:8 * T].rearrange("p (c t) -> p c t", t=T)
        xs = [QX[:], UX[:], U[:, 0:4096].bitcast(F32), U[:, 4096:8192].bitcast(F32)]
        YY = sb("YY", [128, 8, T], F32)
        yo = [YY[:, 0:4, :].rearrange("p a t -> p (a t)"), YY[:, 4:8, :].rearrange("p a t -> p (a t)")]
        tmp = [sb("tmp%d" % i, [128, T], F32) for i in range(4)]
        tmpb = [t[:].bitcast(BF16) for t in tmp]
        qsb = [sb("qsb%d" % i, [128, T], BF16) for i in range(2)]
        kT = sb("kT", [128, 128 + T], BF16)
        kTp = sb("kTp", [128, 2, 128 + T], BF16)
        Vp = sb("Vp", [128, 5, 2, 128], BF16)
        cosT = sb("cosT", [128, T], F32)
        sinT = sb("sinT", [128, T], F32)
        uT = sb("uT", [128, 8, HALO + T], BF16)
        stm = sb("stm", [128, T], F32)
        stv = sb("stv", [128, T], F32)
        ident32 = sb("ident32", [128, 128], F32)
        onesb = sb("onesb", [128, 128], BF16)
        identb = sb("identb", [128, 128], BF16)
        rpermb = sb("rpermb", [128, 128], BF16)
        maskb = sb("maskb", [128, 256], BF16)
        negmb = sb("negmb", [128, 512], BF16)
        pcol = sb("pcol", [128, NPC], F32)
        prow = sb("prow", [128, 144], F32)
        ag = sb("ag", [128, 32], F32)
        sm = sb("sm", [128, 6, 16], F32)
        PS = st.enter_context(nc.psum_tensor("PS", [128, 8, 512], F32))

        mk = lambda n, k: [Slot("%s%d" % (n, i)) for i in range(k)]
        s_R = mk("R", 16); s_xT = mk("xT", 16); s_h = mk("h", NF); s_P = mk("P", 4); s_PT = mk("PT", 4)
        s_D = mk("D", 2); s_mix = mk("mix", 16); s_w = mk("w", 2); s_q = mk("q", 8); s_un = mk("un", 8)
        s_xs = mk("xs", 4); s_Y = mk("Y", 8); s_yo = mk("yo", 2); s_tmp = mk("tmp", 4); s_qsb = mk("qsb", 2)
        s_kT = Slot("kT"); s_V = Slot("V"); s_cos = Slot("cos"); s_sin = Slot("sin"); s_u = mk("u", 8)
        s_uh = Slot("uhalo"); s_stm = Slot("stm"); s_stv = Slot("stv"); s_c = Slot("const"); s_sm = mk("sm", 4); s_B = [Slot("B%d" % i, excl=True) for i in range(8)]
        s_cast = {}
        alias(s_h, s_P + s_PT + s_D + s_mix)
        alias([s_xs[0]], s_q)
        alias([s_xs[1]], s_un)
        alias(s_xs[2:4], s_h + s_P + s_PT)
        alias([s_yo[0]], s_Y[0:4])
        alias([s_yo[1]], s_Y[4:8])

        def cast(name, dst, src):
            s_cast[name] = Slot("cast_" + name)
            if os.environ.get("K_NOCAST"):
                return
            ds = S.dma_sem("c_" + name)
            S.op("pool", lambda e: e.dma_start(out=dst, in_=src), writes=[s_cast[name]], dsem=ds)

        cast("in", winb.rearrange("r (a b) -> r a b", b=832), win_d.rearrange("r (a b) -> r a b", b=832))
        d_cst = S.dma_sem("cst")
        ld_c = [None]

        def load_consts(e):
            return [e.dma_start(out=pcol[:], in_=pcol_d), e.dma_start(out=prow[:], in_=prow_d),
                    e.dma_start(out=tmp[0][:], in_=cst_d)]
        S.op("sp", load_consts, writes=[s_c, s_tmp[0]], dsem=d_cst, ndma=3)

        def mkconst(e):
            e.tensor_copy(out=ident32[:], in_=tmp[0][:, 0:128])
            e.tensor_copy(out=identb[:], in_=tmp[0][:, 0:128])
            e.tensor_copy(out=rpermb[:], in_=tmp[0][:, 128:256])
            e.tensor_copy(out=maskb[:], in_=tmp[0][:, 256:512])
            e.tensor_scalar(out=negmb[:, 0:256], in0=tmp[0][:, 256:512], scalar1=30000.0, scalar2=-30000.0, op0=ALU.mult, op1=ALU.add)
            e.tensor_scalar(out=negmb[:, 256:512], in0=tmp[0][:, 256:512], scalar1=30000.0, scalar2=-30000.0, op0=ALU.mult, op1=ALU.add)
            e.memset(onesb[:], 1.0)
            e.memset(Vp[:], 0.0)
            e.memset(kTp[:], 0.0)
            e.tensor_scalar(out=ag[:, 0:16], in0=pcol[:, PC_G1:PC_G1 + 16], scalar1=ALPHA, scalar2=None, op0=ALU.mult)
            return e.tensor_scalar(out=ag[:, 16:32], in0=pcol[:, PC_B1:PC_B1 + 16], scalar1=ALPHA, scalar2=None, op0=ALU.mult)
        S.op("dve", mkconst, reads=[s_tmp[0]], writes=[s_c, s_V])

        cast("pw", wpwb, wpw_d)
        cast("out", woutb, wout_d)
        GU_PIECES = [(0, 6), (6, 12), (12, 17), (17, 22)]
        for i, (a, b) in enumerate(GU_PIECES):
            cast("gu%d" % i, wgub[:, a * 512:b * 512].rearrange("r (a b) -> r a b", b=512),
                 wgu_d[:, a * 512:b * 512].rearrange("r (a b) -> r a b", b=512))
        for i in range(2):
            cast("dn%d" % i, wdnb[:, i * 1024:(i + 1) * 1024], wdn_d[:, i * 1024:(i + 1) * 1024])

        groups = []
        for _ in range(ntiles):
            for g in (6, 0, 1, 2, 3, 4, 5):
                nw = 512 if g < 6 else 256
                groups.append((winb[:, g * 512:g * 512 + nw].rearrange("(kc p) n -> p kc n", p=128), "in", 16, nw))
            for g in range(2):
                groups.append((wpwb[:, g * 512:(g + 1) * 512].rearrange("(kc p) n -> p kc n", p=128), "pw", 8, 512))
            for g in range(4):
                groups.append((woutb[:, g * 512:(g + 1) * 512].rearrange("(kc p) n -> p kc n", p=128), "out", 16, 512))
            for g in range(22):
                pi = [i for i, (a, b) in enumerate(GU_PIECES) if a <= g < b][0]
                groups.append((wgub[:, g * 512:(g + 1) * 512].rearrange("(kc p) n -> p kc n", p=128), "gu%d" % pi, 16, 512))
            for n in range(4):
                for kg in range(4):
                    groups.append((wdnb[kg * 1408:(kg + 1) * 1408, n * 512:(n + 1) * 512].rearrange("(kc p) n -> p kc n", p=128),
                                   "dn%d" % (n // 2), 11, 512))
        d_w = [S.dma_sem("w%d" % i) for i in range(2)]
        wstate = {"load": 0, "use": 0}

        def w_acquire():
            i = wstate["use"]
            while wstate["load"] < min(len(groups), i + 2):
                li = wstate["load"]
                src, cname, kc, nw = groups[li]
                bi = li % 2
                S.op("sp", lambda e, src=src, kc=kc, nw=nw, bi=bi: e.dma_start(out=wbuf[bi][:, 0:kc, 0:nw], in_=src),
                     reads=[s_cast[cname]], writes=[s_w[bi]], dsem=d_w[bi])
                wstate["load"] += 1
            wstate["use"] += 1
            return wbuf[i % 2], s_w[i % 2]

        d_xs = [S.dma_sem("xs%d" % i) for i in range(4)]
        d_yo = [S.dma_sem("yo%d" % i) for i in range(8)]
        d_pos = S.dma_sem("pos")
        dbg_sems = []

        def pcs(col):
            return pcol[:, col:col + 1]

        bank_rr = {"i": 0}

        def ln_finish(inv_n, b1=2, b2=3):
            S.op("dve", lambda e: e.tensor_scalar(out=stm[:], in0=PS[:, b1, :], scalar1=inv_n, scalar2=None, op0=ALU.mult),
                 reads=[s_B[b1]], writes=[s_stm])
            S.op("dve", lambda e: e.tensor_tensor(out=stv[:], in0=stm[:], in1=stm[:], op=ALU.mult), reads=[s_stm], writes=[s_stv])
            S.op("dve", lambda e: e.scalar_tensor_tensor(out=stv[:], in0=PS[:, b2, :], scalar=inv_n, in1=stv[:], op0=ALU.mult, op1=ALU.subtract),
                 reads=[s_B[b2]], writes=[s_stv])
            S.op("act", lambda e: e.activation(out=stv[:], in_=stv[:], func=AF.Sqrt, bias=EPS_AP[:, 0:1], scale=1.0),
                 reads=[s_c], writes=[s_stv])
            S.op("dve", lambda e: e.reciprocal(out=PS[:, b2, :], in_=stv[:]), reads=[s_stv], writes=[s_B[b2]])
            S.op("dve", lambda e: e.scalar_tensor_tensor(out=PS[:, b1, :], in0=stm[:], scalar=-1.0, in1=PS[:, b2, :], op0=ALU.mult, op1=ALU.mult),
                 reads=[s_stm, s_B[b2]], writes=[s_B[b1]])
            S.op("dve", lambda e: e.tensor_copy(out=stm[:], in_=PS[:, b1, :]), reads=[s_B[b1]], writes=[s_stm])

        def ln_normalize(ap, slot, idx, b1=2, b2=3):
            S.op("dve", lambda e: e.tensor_tensor(out=ap, in0=ap, in1=PS[:, b2, :], op=ALU.mult), reads=[s_B[b2]], writes=[slot])
            if idx % 2 == 0:
                S.op("dve", lambda e: e.tensor_tensor(out=ap, in0=ap, in1=PS[:, b1, :], op=ALU.add), reads=[s_B[b1]], writes=[slot])
            else:
                S.op("pool", lambda e: e.tensor_tensor(out=ap, in0=ap, in1=stm[:], op=ALU.add), reads=[s_stm], writes=[slot])

        EPS_AP = sb("epsap", [128, 1], F32)
        S.op("dve", lambda e: e.memset(EPS_AP[:], EPS), writes=[s_c])

        def stats_mm(src_ap, src_slot, bank, first, last):
            S.op("pe", lambda e: e.matmul(PS[:, bank, :], lhsT=onesb[:], rhs=src_ap, start=first, stop=last),
                 reads=[src_slot, s_c], writes=[s_B[bank]])

        def reduce_angle(a, sa, k, sk):
            S.op("dve", lambda e: e.tensor_scalar(out=k, in0=a, scalar1=float(1.0 / TWO_PI), scalar2=MAGIC, op0=ALU.mult, op1=ALU.add),
                 reads=[sa], writes=[sk])
            S.op("dve", lambda e: e.tensor_scalar(out=k, in0=k, scalar1=-MAGIC, scalar2=None, op0=ALU.add), writes=[sk])
            S.op("dve", lambda e: e.scalar_tensor_tensor(out=a, in0=k, scalar=-C1, in1=a, op0=ALU.mult, op1=ALU.add), reads=[sk], writes=[sa])
            S.op("dve", lambda e: e.scalar_tensor_tensor(out=a, in0=k, scalar=-C2, in1=a, op0=ALU.mult, op1=ALU.add), reads=[sk], writes=[sa])
            S.op("dve", lambda e: e.tensor_scalar(out=a, in0=a, scalar1=PI_LO, scalar2=-PI_LO, op0=ALU.min, op1=ALU.max), writes=[sa])

        def dbg_dump(name, ap, slots):
            if not debug:
                return
            ds = S.dma_sem("dbg_" + name)
            dbg_sems.append(ds)
            S.op("sp", lambda e: e.dma_start(out=dbg_out[name], in_=ap), reads=slots, writes=[], dsem=ds)

        def load_x_block(tok0, b, bi):
            S.op("sp", lambda e: e.dma_start(out=xs[bi], in_=x_d[tok0 + b * 128: tok0 + (b + 1) * 128, :]),
                 writes=[s_xs[bi]], dsem=d_xs[bi])

        def gen_tables(tix):
            sq_, tt_ = tix // NTILE_SEQ, tix % NTILE_SEQ
            S.op("sp", lambda e: e.dma_start(out=sinT[:].bitcast(I32),
                                             in_=pos_d[sq_:sq_ + 1, tt_ * T:(tt_ + 1) * T].partition_broadcast(128)),
                 writes=[s_sin], dsem=d_pos)
            S.op("dve", lambda e: e.tensor_copy(out=cosT[:], in_=sinT[:].bitcast(I32)), reads=[s_sin], writes=[s_cos])
            S.op("dve", lambda e: e.tensor_scalar(out=cosT[:], in0=cosT[:], scalar1=pcs(PC_INVF), scalar2=None, op0=ALU.mult),
                 reads=[s_c], writes=[s_cos])
            S.op("dve", lambda e: e.tensor_scalar(out=sinT[:], in0=cosT[:], scalar1=float(np.pi / 2), scalar2=None, op0=ALU.add),
                 reads=[s_cos], writes=[s_sin])
            reduce_angle(cosT[:], s_cos, tmp[2][:], s_tmp[2])
            reduce_angle(sinT[:], s_sin, tmp[2][:], s_tmp[2])
            S.op("act", lambda e: e.activation(out=tmp[3][:], in_=cosT[:], func=AF.Sin, scale=pcs(PC_SIGN)),
                 reads=[s_cos, s_c], writes=[s_tmp[3]])
            S.op("act", lambda e: e.activation(out=cosT[:], in_=sinT[:], func=AF.Sin), reads=[s_sin], writes=[s_cos])
            S.op("act", lambda e: e.activation(out=sinT[:], in_=tmp[3][:], func=AF.Identity), reads=[s_tmp[3]], writes=[s_sin])

        pend_stats = []

        def flush_stats():
            while pend_stats:
                pend_stats.pop(0)()

        def resid_stats(dc, bank, b1=2, b2=3, bias_col=None):
            if bias_col is None:
                S.op("dve", lambda e: e.tensor_tensor(out=R[:, dc, :], in0=PS[:, bank, :], in1=R[:, dc, :], op=ALU.add),
                     reads=[s_B[bank]], writes=[s_R[dc]])
            else:
                S.op("dve", lambda e: e.scalar_tensor_tensor(out=R[:, dc, :], in0=PS[:, bank, :], scalar=pcs(bias_col), in1=R[:, dc, :],
                                                             op0=ALU.add, op1=ALU.add), reads=[s_B[bank], s_c], writes=[s_R[dc]])
            sq = dc % 4
            while len(pend_stats) >= 4:
                pend_stats.pop(0)()

            def sqy(e):
                e.activation(out=tmpb[sq][:, 0:T], in_=R[:, dc, :], func=AF.Square)
                return e.activation(out=tmpb[sq][:, T:2 * T], in_=R[:, dc, :], func=AF.Identity)
            S.op("act", sqy, reads=[s_R[dc]], writes=[s_tmp[sq]])

            def later():
                stats_mm(tmpb[sq][:, T:2 * T], s_tmp[sq], b1, dc == 0, dc == 15)
                stats_mm(tmpb[sq][:, 0:T], s_tmp[sq], b2, dc == 0, dc == 15)
            pend_stats.append(later)

        def kouter_group(wb, wslot, src, src_slots, K, banks):
            for kc in range(K):
                def mm(e, kc=kc):
                    r = None
                    for cc in range(4):
                        r = e.matmul(PS[:, banks[cc], :], lhsT=wb[:, kc, cc * 128:(cc + 1) * 128], rhs=src[:, kc, :],
                                     start=(kc == 0), stop=(kc == K - 1))
                    return r
                S.op("pe", mm, reads=[wslot, src_slots[kc]], writes=[s_B[b] for b in banks])

        for ti in range(ntiles):
            seq = ti // NTILE_SEQ
            tt = ti % NTILE_SEQ
            tok0 = seq * SEQ + tt * T
            first = tt == 0
            dbg_here = debug and ti == 0

            if upto < -2:
                continue
            if ti == 0:
                gen_tables(0)

            if upto < -1:
                continue
            for b in range(4):
                bi = b
                if ti == 0:
                    load_x_block(tok0, b, bi)
                for g in range(4):
                    bank = g

                    def tr(e, g=g, bi=bi, bank=bank):
                        r = None
                        for i in range(4):
                            kc = 4 * g + i
                            r = e.transpose(out=PS[:, bank, i * 128:(i + 1) * 128], in_=xs[bi][:, kc * 128:(kc + 1) * 128],
                                            identity=ident32[:])
                        return r
                    S.op("pe", tr, reads=[s_xs[bi], s_c], writes=[s_B[bank]])

                    S.op("act", lambda e, g=g, b=b, bank=bank: e.activation(
                        out=R[:, 4 * g:4 * g + 4, b * 128:(b + 1) * 128], in_=PS[:, bank, :].rearrange("p (i t) -> p i t", t=128),
                        func=AF.Identity, scale=ALPHA), reads=[s_B[bank]], writes=s_R[4 * g:4 * g + 4])
                    S.op("dve", lambda e, g=g, b=b, bank=bank: e.tensor_copy(
                        out=xT[:, 4 * g:4 * g + 4, b * 128:(b + 1) * 128],
                        in_=PS[:, bank, :].rearrange("p (i t) -> p i t", t=128)),
                        reads=[s_B[bank]], writes=s_xT[4 * g:4 * g + 4])

            if upto < 0:
                continue
            if first:
                S.op("dve", lambda e: e.memset(uT[:, :, 0:HALO], 0.0), writes=[s_uh])
            acc_banks = [2, 3, 4]
            rot_banks = [5, 6]
            ctr = {"acc": 0, "rot": 0, "tmp": 0}

            def inproj_mm(wb, wslot, cc, bank):
                def mm(e, wb=wb, cc=cc, bank=bank):
                    r = None
                    for kc in range(16):
                        r = e.matmul(PS[:, bank, :], lhsT=wb[:, kc, cc * 128:(cc + 1) * 128], rhs=xT[:, kc, :],
                                     start=(kc == 0), stop=(kc == 15))
                    return r
                S.op("pe", mm, reads=[wslot] + s_xT, writes=[s_B[bank]])

            pend_rot = []

            def flush_rot():
                while pend_rot:
                    pend_rot.pop(0)()

            def qk_chunk(wb, wslot, cc, ch):
                bank = acc_banks[ctr["acc"] % 3]
                ctr["acc"] += 1
                inproj_mm(wb, wslot, cc, bank)
                qi = ctr["rot"] % 2
                rb = rot_banks[ctr["rot"] % 2]
                ctr["rot"] += 1
                t1 = 2 + (ctr["tmp"] % 2)
                ctr["tmp"] += 1
                t2 = 0 if t1 == 2 else 1
                flush_rot()
                S.op("act", lambda e: e.activation(out=qsb[qi][:], in_=PS[:, bank, :], func=AF.Identity, bias=pcs(PC_BIN + ch), scale=1.0),
                     reads=[s_B[bank], s_c], writes=[s_qsb[qi]])
                if ch < 8:
                    dst, dslot = qT[:, ch, :], s_q[ch]
                else:
                    dst, dslot = kT[:, 128:128 + T], s_kT

                def later():
                    S.op("pe", lambda e: e.matmul(PS[:, rb, :], lhsT=rpermb[:], rhs=qsb[qi][:], start=True, stop=True),
                         reads=[s_qsb[qi], s_c], writes=[s_B[rb]])
                    S.op("pool", lambda e: e.tensor_tensor(out=tmp[t1][:], in0=qsb[qi][:], in1=cosT[:], op=ALU.mult),
                         reads=[s_qsb[qi], s_cos], writes=[s_tmp[t1]])
                    S.op("dve", lambda e: e.tensor_tensor(out=tmp[t2][:], in0=PS[:, rb, :], in1=sinT[:], op=ALU.mult),
                         reads=[s_B[rb], s_sin], writes=[s_tmp[t2]])
                    if ch < 8:
                        S.op("dve", lambda e: e.tensor_tensor(out=dst, in0=tmp[t1][:], in1=tmp[t2][:], op=ALU.add),
                             reads=[s_tmp[t1], s_tmp[t2]], writes=[dslot])
                    else:
                        def kadd(e):
                            e.tensor_tensor(out=dst, in0=tmp[t1][:], in1=tmp[t2][:], op=ALU.add)
                            e.tensor_tensor(out=kTp[0:64, 0, 128:128 + T], in0=tmp[t1][0:64, :], in1=tmp[t2][0:64, :], op=ALU.add)
                            return e.tensor_tensor(out=kTp[64:128, 1, 128:128 + T], in0=tmp[t1][64:128, :], in1=tmp[t2][64:128, :], op=ALU.add)
                        S.op("dve", kadd, reads=[s_tmp[t1], s_tmp[t2]], writes=[dslot])
                pend_rot.append(later)

            wb, wslot = w_acquire()
            qk_chunk(wb, wslot, 0, 24)
            for b in range(4):
                vb = 7 if b % 2 == 0 else 0

                def mmv(e, wb=wb, b=b, vb=vb):
                    r = None
                    for kc in range(16):
                        r = e.matmul(PS[:, vb, 0:128], lhsT=xT[:, kc, b * 128:(b + 1) * 128], rhs=wb[:, kc, 128:256],
                                     start=(kc == 0), stop=(kc == 15))
                    return r
                S.op("pe", mmv, reads=[wslot] + s_xT, writes=[s_B[vb]])
                flush_rot()

                def evv(e, b=b, vb=vb):
                    e.tensor_tensor(out=Vp[:, b + 1, 0, 0:64], in0=PS[:, vb, 0:64], in1=prow[:, 0:64], op=ALU.add)
                    return e.tensor_tensor(out=Vp[:, b + 1, 1, 64:128], in0=PS[:, vb, 64:128], in1=prow[:, 64:128], op=ALU.add)
                S.op("dve", evv, reads=[s_B[vb], s_c], writes=[s_V])
            for g in range(2):
                wb, wslot = w_acquire()
                for cc in range(4):
                    qk_chunk(wb, wslot, cc, 4 * g + cc)
            flush_rot()
            if dbg_here:
                dbg_dump("d_q", qT.rearrange("p c t -> p (c t)"), s_q)
                dbg_dump("d_k", kT[:, 128:128 + T], [s_kT])

            wstate2 = {}

            def pair_item(c):
                if c % 2 == 0:
                    wstate2["w"] = w_acquire()
                wb, wslot = wstate2["w"]
                cc0 = (c % 2) * 2
                sgi = c % 2
                inproj_mm(wb, wslot, cc0, 4)
                S.op("act", lambda e: e.activation(out=tmp[sgi][:], in_=PS[:, 4, :], func=AF.Sigmoid, bias=pcs(PC_BIN + 8 + 2 * c), scale=1.0),
                     reads=[s_B[4], s_c], writes=[s_tmp[sgi]])
                inproj_mm(wb, wslot, cc0 + 1, 5)
                S.op("dve", lambda e: e.scalar_tensor_tensor(out=uT[:, c, HALO:HALO + T], in0=PS[:, 5, :], scalar=pcs(PC_BIN + 9 + 2 * c),
                                                             in1=tmp[sgi][:], op0=ALU.add, op1=ALU.mult),
                     reads=[s_B[5], s_tmp[sgi], s_c], writes=[s_u[c]])

            def conv_prep(c):
                di = c % 2

                def dgen(e):
                    r = None
                    for jj in range(KW):
                        r = e.tensor_scalar(out=Dg[di][:, jj, :], in0=identb[:], scalar1=pcs(PC_WDW + c * KW + jj), scalar2=None, op0=ALU.mult)
                    return r
                S.op("dve", dgen, reads=[s_c], writes=[s_D[di]])

            def conv_item(c):
                di = c % 2
                bank = 4 + c % 2
                sq = 2 + c % 2

                def cv(e):
                    r = None
                    for jj in range(KW):
                        r = e.matmul(PS[:, bank, :], lhsT=Dg[di][:, jj, :], rhs=uT[:, c, jj:jj + T], start=(jj == 0), stop=(jj == KW - 1))
                    return r
                S.op("pe", cv, reads=[s_D[di], s_u[c], s_uh], writes=[s_B[bank]])
                S.op("act", lambda e: e.activation(out=YY[:, c, :], in_=PS[:, bank, :], func=AF.Identity, bias=pcs(PC_BDW + c), scale=1.0),
                     reads=[s_B[bank], s_c], writes=[s_Y[c]])

                def sqy(e):
                    e.activation(out=tmpb[sq][:, 0:T], in_=PS[:, bank, :], func=AF.Square, bias=pcs(PC_BDW + c), scale=1.0)
                    return e.activation(out=tmpb[sq][:, T:2 * T], in_=PS[:, bank, :], func=AF.Identity, bias=pcs(PC_BDW + c), scale=1.0)
                S.op("act", sqy, reads=[s_B[bank], s_c], writes=[s_tmp[sq]])
                def later():
                    stats_mm(tmpb[sq][:, T:2 * T], s_tmp[sq], 6, c == 0, c == 7)
                    stats_mm(tmpb[sq][:, 0:T], s_tmp[sq], 7, c == 0, c == 7)
                pend_stats.append(later)

            items = []
            for c in range(8):
                items.append(("pair", c))
                if c >= 1:
                    items.append(("conv", c - 1))
            items.append(("conv", 7))
            items.append(("cln", 0))
            for oc in range(8):
                items.append(("pw", oc))

            def cln_item():
                flush_stats()
                ln_finish(1.0 / 1024, 6, 7)
                for c in range(8):
                    ln_normalize(YY[:, c, :], s_Y[c], c, 6, 7)
                    S.op("act", lambda e, c=c: e.activation(out=un[:, c, :], in_=YY[:, c, :], func=AF.Silu, bias=pcs(PC_CLB + c),
                                                            scale=pcs(PC_CLG + c)),
                         reads=[s_Y[c], s_c], writes=[s_un[c]])

            def pw_item(oc):
                if oc % 4 == 0:
                    wstate2["pw"] = w_acquire()
                wb, wslot = wstate2["pw"]
                cc = oc % 4
                bank = 4 + oc % 2

                def mm(e):
                    r = None
                    for kc in range(8):
                        r = e.matmul(PS[:, bank, :], lhsT=wb[:, kc, cc * 128:(cc + 1) * 128], rhs=un[:, kc, :], start=(kc == 0), stop=(kc == 7))
                    return r
                S.op("pe", mm, reads=[wslot] + s_un, writes=[s_B[bank]])
                S.op("act", lambda e: e.activation(out=mixT[:, 8 + oc, :], in_=PS[:, bank, :], func=AF.Identity,
                                                   bias=pcs(PC_BPW + oc), scale=1.0),
                     reads=[s_B[bank], s_c], writes=[s_mix[8 + oc]])

            def run_item():
                if not items:
                    return
                kind, c = items.pop(0)
                if kind == "pair":
                    pair_item(c)
                    flush_stats()
                elif kind == "cln":
                    cln_item()
                elif kind == "pw":
                    pw_item(c)
                else:
                    todo = list(pend_stats)
                    del pend_stats[:]
                    conv_item(c)
                    for fn in todo:
                        fn()
                for k2, c2 in items[:2]:
                    if k2 == "conv" and c2 not in prepped:
                        conv_prep(c2)
                        prepped.add(c2)
                        break
            prepped = set()
            conv_prep(0)
            prepped.add(0)

            pend_T = []
            pend_V = []
            rmax, mmx, negm, Ex, sums, rden = [sm[:, i, :] for i in range(6)]
            sinkrow = prow[:, 128:144]
            for j in range(4):
                nb = tt * 4 + j
                k0 = 0 if nb > 0 else 128
                L = 256 - k0
                kb0 = k0 // 128
                for sg in range(4):
                    bA = 0
                    ptb = 2
                    psc = PS[:, bA:bA + 2, :].rearrange("p b (s k) -> p (b s) k", k=256)
                    sl = slice(4 * sg, 4 * sg + 4)

                    def sc(e, sg=sg, j=j, k0=k0, bA=bA):
                        r = None
                        for rr in range(2):
                            e.matmul(PS[:, bA + rr, :], lhsT=identb[:], rhs=negmb[:], start=True, stop=False)
                        for rr in range(2):
                            for ccx in range(2):
                                c = 2 * sg + ccx
                                col = ccx * 256
                                r = e.matmul(PS[:, bA + rr, col + k0:col + 256], lhsT=qT[:, c, j * 128:(j + 1) * 128],
                                             rhs=kTp[:, rr, j * 128 + k0:j * 128 + 256], start=False, stop=(ccx == 1))
                        return r
                    S.op("pe", sc, reads=[s_q[2 * sg], s_q[2 * sg + 1], s_kT, s_c], writes=[s_B[bA], s_B[bA + 1]])
                    while pend_T:
                        pend_T.pop(0)()

                    S.op("dve", lambda e, psc=psc, sl=sl, k0=k0: e.tensor_reduce(out=rmax[:, sl], in_=psc[:, :, k0:256], axis=AX.X, op=ALU.max),
                         reads=[s_B[bA], s_B[bA + 1]], writes=[s_sm[sg]])
                    S.op("dve", lambda e, sl=sl: e.scalar_tensor_tensor(out=mmx[:, sl], in0=rmax[:, sl], scalar=0.125, in1=sinkrow[:, sl],
                                                                          op0=ALU.mult, op1=ALU.max), reads=[s_c], writes=[s_sm[sg]])
                    S.op("dve", lambda e, sl=sl: e.tensor_scalar(out=negm[:, sl], in0=mmx[:, sl], scalar1=-1.0, scalar2=None, op0=ALU.mult),
                         writes=[s_sm[sg]])
                    S.op("dve", lambda e, sl=sl: e.tensor_tensor(out=Ex[:, sl], in0=negm[:, sl], in1=sinkrow[:, sl], op=ALU.add),
                         reads=[s_c], writes=[s_sm[sg]])

                    def ex(e, psc=psc, sg=sg, k0=k0, sl=sl):
                        for n in range(4):
                            s = 4 * sg + n
                            e.activation(out=P[:, s, k0:256], in_=psc[:, n, k0:256], func=AF.Exp, bias=negm[:, s:s + 1], scale=0.125,
                                         accum_out=sums[:, s:s + 1])
                        return e.activation(out=Ex[:, sl], in_=Ex[:, sl], func=AF.Exp)
                    S.op("act", ex, reads=[s_B[bA], s_B[bA + 1], s_sm[sg]], writes=[s_P[sg], s_sm[sg]])

                    S.op("dve", lambda e, sl=sl: e.tensor_tensor(out=sums[:, sl], in0=sums[:, sl], in1=Ex[:, sl], op=ALU.add), writes=[s_sm[sg]])
                    S.op("dve", lambda e, sl=sl: e.reciprocal(out=rden[:, sl], in_=sums[:, sl]), writes=[s_sm[sg]])
                    S.op("dve", lambda e, sl=sl, k0=k0, L=L: e.tensor_tensor(
                        out=P[:, sl, k0:256], in0=P[:, sl, k0:256],
                        in1=rden[:, sl].unsqueeze(2).broadcast_to([128, 4, L]), op=ALU.mult), reads=[s_sm[sg]], writes=[s_P[sg]])

                    if items and items[0][0] in ("pair", "conv"):
                        run_item()
                    while pend_V:
                        pend_V.pop(0)()

                    ptps = PS[:, ptb, :].bitcast(BF16)

                    def tpart(sg=sg, kb0=kb0, ptps=ptps, j=j):
                        def trp(e, sg=sg, kb0=kb0, ptps=ptps):
                            r = None
                            for kb in range(kb0, 2):
                                for n in range(4):
                                    idx = kb * 4 + n
                                    r = e.transpose(out=ptps[:, idx * 128:(idx + 1) * 128], in_=P[:, 4 * sg + n, kb * 128:(kb + 1) * 128],
                                                    identity=identb[:])
                            return r
                        S.op("pe", trp, reads=[s_P[sg], s_c], writes=[s_B[ptb]])
                        S.op("act", lambda e, sg=sg, kb0=kb0, ptps=ptps: e.activation(
                            out=PTv[:, 2 * kb0:4, 2 * sg:2 * sg + 2, :],
                            in_=ptps[:, kb0 * 512:1024].rearrange("p (a c q) -> p a c q", c=2, q=128), func=AF.Identity),
                            reads=[s_B[ptb]], writes=[s_PT[sg]])
                    def vpart(sg=sg, kb0=kb0, j=j):
                        if sg % 2 == 1:
                            cg = sg // 2
                            ob = 3

                            def pv(e, cg=cg, ob=ob, j=j, kb0=kb0):
                                r = None
                                n = 0
                                tot = (2 - kb0) * 2
                                for kb in range(kb0, 2):
                                    for rr in range(2):
                                        r = e.matmul(PS[:, ob, :], lhsT=Vp[:, j + kb, rr, :], rhs=PTv[:, kb * 2 + rr, 4 * cg:4 * cg + 4, :],
                                                     start=(n == 0), stop=(n == tot - 1))
                                        n += 1
                                return r
                            S.op("pe", pv, reads=[s_V, s_PT[sg - 1], s_PT[sg]], writes=[s_B[ob]])
                            S.op("dve", lambda e, cg=cg, ob=ob, j=j: e.tensor_copy(
                                out=mixT[:, 4 * cg:4 * cg + 4, j * 128:(j + 1) * 128], in_=PS[:, ob, :].rearrange("p (c q) -> p c q", q=128)),
                                reads=[s_B[ob]], writes=s_mix[4 * cg:4 * cg + 4])
                    pend_T.append(tpart)
                    pend_T.append(lambda vp=vpart: pend_V.append(vp))
            while pend_T:
                pend_T.pop(0)()
            while pend_V:
                pend_V.pop(0)()
            while items:
                run_item()
            flush_stats()
            if dbg_here:
                dbg_dump("d_u", uT[:].rearrange("p c t -> p (c t)"), s_u + [s_uh])

            if tt < NTILE_SEQ - 1:
                def carry(e):
                    e.tensor_copy(out=uT[:, :, 0:HALO], in_=uT[:, :, T:T + HALO])
                    e.tensor_copy(out=kTp[:, :, 0:128], in_=kTp[:, :, T:T + 128])
                    return e.tensor_copy(out=Vp[:, 0, :, :], in_=Vp[:, 4, :, :])
                S.op("dve", carry, reads=s_u, writes=[s_uh, s_kT, s_V])
            if dbg_here:
                dbg_dump("d_mix", mixT.rearrange("p c t -> p (c t)"), s_mix)

            if upto < 3:
                continue
            accb = [0, 1, 4]
            na = 0
            for g in range(4):
                wb, wslot = w_acquire()
                for cc in range(4):
                    dc = 4 * g + cc
                    bank = accb[na % 3]
                    na += 1

                    def mm(e, wb=wb, cc=cc, bank=bank):
                        r = None
                        for kc in range(16):
                            r = e.matmul(PS[:, bank, :], lhsT=wb[:, kc, cc * 128:(cc + 1) * 128], rhs=mixT[:, kc, :], start=(kc == 0), stop=(kc == 15))
                        return r
                    S.op("pe", mm, reads=[wslot] + s_mix, writes=[s_B[bank]])
                    flush_stats()
                    resid_stats(dc, bank, bias_col=PC_BOUT + dc)
            flush_stats()
            ln_finish(1.0 / D)
            for dc in range(16):
                ln_normalize(R[:, dc, :], s_R[dc], dc)
                S.op("act", lambda e, dc=dc: e.activation(out=xT[:, dc, :], in_=R[:, dc, :], func=AF.Identity, bias=pcs(PC_B1 + dc),
                                                          scale=pcs(PC_G1 + dc)), reads=[s_c, s_R[dc]], writes=[s_xT[dc]])
                S.op("act", lambda e, dc=dc: e.activation(out=R[:, dc, :], in_=R[:, dc, :], func=AF.Identity, bias=ag[:, 16 + dc:17 + dc],
                                                          scale=ag[:, dc:dc + 1]), reads=[s_c], writes=[s_R[dc]])
            if dbg_here:
                dbg_dump("d_x1", R[:].rearrange("p c t -> p (c t)"), s_R)

            if upto < 4:
                continue
            accb = [0, 1, 4, 5]
            na = 0
            for g in range(22):
                wb, wslot = w_acquire()
                if g == 4 and ti + 1 < ntiles:
                    gen_tables(ti + 1)
                if g == 0:
                    kouter_group(wb, wslot, xT, s_xT, 16, accb)
                for cc in range(4):
                    f = 2 * g + cc // 2
                    bank = accb[na % 4]
                    na += 1

                    def mm(e, wb=wb, cc=cc, bank=bank):
                        r = None
                        for kc in range(16):
                            r = e.matmul(PS[:, bank, :], lhsT=wb[:, kc, cc * 128:(cc + 1) * 128], rhs=xT[:, kc, :], start=(kc == 0), stop=(kc == 15))
                        return r
                    if g > 0:
                        S.op("pe", mm, reads=[wslot] + s_xT, writes=[s_B[bank]])
                    sgi = f % 2
                    if cc % 2 == 0:
                        S.op("act", lambda e, bank=bank, sgi=sgi: e.activation(out=tmp[sgi][:], in_=PS[:, bank, :], func=AF.Silu),
                             reads=[s_B[bank]], writes=[s_tmp[sgi]])
                    else:
                        S.op("dve", lambda e, bank=bank, sgi=sgi, f=f: e.tensor_tensor(out=hT[:, f, :], in0=PS[:, bank, :], in1=tmp[sgi][:], op=ALU.mult),
                             reads=[s_B[bank], s_tmp[sgi]], writes=[s_h[f]])

            if upto < 5:
                continue
            if ti + 1 < ntiles:
                nseq = (ti + 1) // NTILE_SEQ
                ntok0 = nseq * SEQ + ((ti + 1) % NTILE_SEQ) * T
                for b in range(2):
                    load_x_block(ntok0, b, b)
            accb = [0, 1, 4, 5]
            for n in range(4):
                for kg in range(4):
                    wb, wslot = w_acquire()
                    for cc in range(4):
                        bank = accb[cc]

                        def mm(e, wb=wb, cc=cc, bank=bank, kg=kg):
                            r = None
                            for k in range(11):
                                r = e.matmul(PS[:, bank, :], lhsT=wb[:, k, cc * 128:(cc + 1) * 128], rhs=hT[:, kg * 11 + k, :],
                                             start=(kg == 0 and k == 0), stop=(kg == 3 and k == 10))
                            return r
                        S.op("pe", mm, reads=[wslot] + s_h[kg * 11:(kg + 1) * 11], writes=[s_B[bank]])
                    if kg == 0:
                        flush_stats()
                for cc in range(4):
                    resid_stats(4 * n + cc, accb[cc])
            flush_stats()
            if ti + 1 < ntiles:
                for b in range(2, 4):
                    load_x_block(ntok0, b, b)
            ln_finish(1.0 / D)

            def norm_group(g):
                for dc in range(4 * g, 4 * g + 4):
                    ln_normalize(R[:, dc, :], s_R[dc], dc)
                    S.op("act", lambda e, dc=dc: e.activation(out=R[:, dc, :], in_=R[:, dc, :], func=AF.Identity,
                                                              bias=pcs(PC_B2 + dc), scale=pcs(PC_G2 + dc)),
                         reads=[s_c], writes=[s_R[dc]])
            norm_group(0)
            for g in range(4):
                if g + 1 < 4:
                    norm_group(g + 1)
                for b in range(4):
                    k = 4 * g + b
                    bank = 6 + k % 2
                    pi = k % 8

                    def trf(e, g=g, b=b, bank=bank):
                        r = None
                        for i in range(4):
                            dc = 4 * g + i
                            r = e.transpose(out=PS[:, bank, i * 128:(i + 1) * 128], in_=R[:, dc, b * 128:(b + 1) * 128], identity=ident32[:])
                        return r
                    S.op("pe", trf, reads=s_R[4 * g:4 * g + 4] + [s_c], writes=[s_B[bank]])
                    if k % 2 == 0:
                        S.op("act", lambda e, pi=pi, bank=bank: e.activation(out=YY[:, pi, :], in_=PS[:, bank, :], func=AF.Identity),
                             reads=[s_B[bank]], writes=[s_Y[pi]])
                    else:
                        S.op("dve", lambda e, pi=pi, bank=bank: e.tensor_copy(out=YY[:, pi, :], in_=PS[:, bank, :]),
                             reads=[s_B[bank]], writes=[s_Y[pi]])
                    S.op("pool", lambda e, b=b, g=g, pi=pi, tok0=tok0: e.dma_start(
                        out=out_d[tok0 + b * 128: tok0 + (b + 1) * 128, g * 512:(g + 1) * 512], in_=YY[:, pi, :]),
                        reads=[s_Y[pi]], writes=[], dsem=d_yo[pi])

        fw = [(d, S.cnt[d]) for d in d_yo]
        for ds in dbg_sems:
            fw.append((ds, S.cnt[ds]))
        S.emit(fw)
    return nc


def _prep_shared(inp):
    f32 = np.float32
    qcols = []
    for c in range(8):
        qcols += list(range(c * 64, c * 64 + 64)) + list(range((8 + c) * 64, (8 + c) * 64 + 64))
    kcols = list(range(1024, 1152))
    vcols = list(range(1152, 1280))
    a0, g0 = 1280, 2304
    conv = []
    for c in range(8):
        conv += list(range(g0 + c * 128, g0 + (c + 1) * 128)) + list(range(a0 + c * 128, a0 + (c + 1) * 128))
    perm = np.array(qcols + conv + kcols + vcols)
    w_in = np.ascontiguousarray(np.asarray(inp["w_in"])[0][:, perm], dtype=f32)
    b_in = np.asarray(inp["b_in"])[0][perm].astype(f32)
    rows = np.array(qcols + list(range(1024, 2048)))
    w_out = np.ascontiguousarray(np.asarray(inp["w_out"])[0][rows, :], dtype=f32)
    wg = np.asarray(inp["w_gate"])[0].reshape(D, NF, 1, 128)
    wu = np.asarray(inp["w_up"])[0].reshape(D, NF, 1, 128)
    w_gu = np.ascontiguousarray(np.concatenate([wg, wu], axis=2).reshape(D, 2 * DFF), dtype=f32)
    w_down = np.ascontiguousarray(np.asarray(inp["w_down"])[0], dtype=f32)
    w_pw2 = np.ascontiguousarray(np.asarray(inp["w_pw2"])[0], dtype=f32)

    pcol = np.zeros((128, NPC), f32)
    pcol[:, PC_BIN:PC_BIN + 25] = b_in[:3200].reshape(25, 128).T
    col = lambda v, n: np.asarray(v)[0].astype(f32).reshape(n, 128).T
    pcol[:, PC_BDW:PC_BDW + 8] = col(inp["b_dw"], 8)
    pcol[:, PC_CLG:PC_CLG + 8] = col(inp["conv_ln_g"], 8)
    pcol[:, PC_CLB:PC_CLB + 8] = col(inp["conv_ln_b"], 8)
    pcol[:, PC_BPW:PC_BPW + 8] = col(inp["b_pw2"], 8)
    pcol[:, PC_BOUT:PC_BOUT + 16] = col(inp["b_out"], 16)
    pcol[:, PC_G1:PC_G1 + 16] = col(inp["ln1_g"], 16)
    pcol[:, PC_B1:PC_B1 + 16] = col(inp["ln1_b"], 16)
    pcol[:, PC_G2:PC_G2 + 16] = col(inp["ln2_g"], 16)
    pcol[:, PC_B2:PC_B2 + 16] = col(inp["ln2_b"], 16)
    half = 32
    inv_freq = (1.0 / (np.float32(10000.0) ** (np.arange(half, dtype=f32) * np.float32(2.0) / np.float32(64)))).astype(f32)
    p = np.arange(128)
    pcol[:, PC_INVF] = inv_freq[p % 32]
    pcol[:, PC_SIGN] = np.where((p % 64) < 32, -1.0, 1.0)
    wdw = np.asarray(inp["w_dw"])[0][:, 0, :].astype(f32)
    pcol[:, PC_WDW:] = wdw.reshape(KW, 8, 128).transpose(2, 1, 0).reshape(128, 8 * KW)

    prow = np.zeros((128, 144), f32)
    prow[:, 0:128] = b_in[3200:3328][None, :]
    sinks = np.asarray(inp["sinks"])[0].astype(f32)
    slot = np.array([sinks[(2 * (s // 4) + (s % 2)) + 8 * ((s % 4) // 2)] for s in range(16)], f32)
    prow[:, 128:144] = slot[None, :]

    cst = np.zeros((128, 512), f32)
    cst[:, 0:128] = np.eye(128, dtype=f32)
    m = np.arange(128)
    partner = np.where((m % 64) < 32, m + 32, m - 32)
    cst[partner, 128 + m] = 1.0
    qi = np.arange(128)[:, None]
    kj = np.arange(128)[None, :]
    cst[:, 256:384] = (kj > qi).astype(f32)
    cst[:, 384:512] = (kj <= qi).astype(f32)
    return dict(w_in=w_in, w_pw2=w_pw2, w_out=w_out, w_gu=w_gu, w_down=w_down, pcol=pcol, prow=prow, cst=cst)


def kernel(**inputs):
    shared = _prep_shared(inputs)
    x = np.asarray(inputs["x"], dtype=np.float32)
    pos = np.asarray(inputs["positions"]).astype(np.int32)
    nc = build_nc()
    in_maps = []
    for c in range(NCORES):
        m = dict(shared)
        m["x"] = np.ascontiguousarray(x[2 * c:2 * c + 2].reshape(2 * SEQ, D))
        m["pos"] = np.ascontiguousarray(pos[2 * c:2 * c + 2])
        in_maps.append(m)
    res = run_bass_kernel_spmd(nc, in_maps, core_ids=list(range(NCORES)))
    out = np.concatenate([np.asarray(r["out"]).reshape(2, SEQ, D) for r in res.results], axis=0)
    return out.astype(np.float32)
```

```python
import os
import numpy as np
from contextlib import ExitStack
import concourse.bass as bass
import concourse.mybir as mybir
from concourse.bass_utils import run_bass_kernel_spmd

F32 = mybir.dt.float32
BF16 = mybir.dt.bfloat16
I32 = mybir.dt.int32
AF = mybir.ActivationFunctionType
ALU = mybir.AluOpType
AX = mybir.AxisListType

NCORES = 8
D = 2048
SEQ = 2048
T = 512
NTILE_SEQ = SEQ // T
DFF = 5632
NF = DFF // 128
KW = 31
HALO = KW - 1
ALPHA = float(2.0 ** 0.25)
EPS = 1e-5
MAGIC = 12582912.0
TWO_PI = float(2 * np.pi)
C1 = 6.28125
C2 = float(2 * np.pi - 6.28125)
PI_LO = 3.1415925

PC_BIN = 0
PC_BDW = 25
PC_CLG = 33
PC_CLB = 41
PC_BPW = 49
PC_BOUT = 57
PC_G1 = 73
PC_B1 = 89
PC_G2 = 105
PC_B2 = 121
PC_INVF = 137
PC_SIGN = 138
PC_WDW = 139
NPC = PC_WDW + 8 * KW


class Slot:
    __slots__ = ("name", "w", "r", "conf", "excl")

    def __init__(self, name, excl=False):
        self.name = name
        self.excl = excl
        self.w = None
        self.r = {}
        self.conf = []


def alias(group_a, group_b):
    for a in group_a:
        for b in group_b:
            if b not in a.conf:
                a.conf.append(b)
            if a not in b.conf:
                b.conf.append(a)


class Sched:
    def __init__(self, nc, stack, same_engine_sync=True):
        self.nc = nc
        self.q = {k: [] for k in ("pe", "act", "dve", "pool", "sp")}
        self.cnt = {}
        self.semobj = {}
        for k in ("pe", "act", "dve", "pool"):
            self.semobj[("eng", k)] = stack.enter_context(nc.semaphore("sem_" + k))
            self.cnt[k] = 0
        self.observed = {k: {} for k in self.q}
        self.same = same_engine_sync
        self.stack = stack

    def dma_sem(self, name):
        key = ("dma", name)
        self.semobj[key] = self.stack.enter_context(self.nc.semaphore("dsem_" + name))
        self.cnt[key] = 0
        return key

    def op(self, eng, fn, reads=(), writes=(), dsem=None, ndma=1):
        deps = {}
        writes = list(writes) + [s for s in reads if s.excl]
        reads = [s for s in reads if not s.excl]

        def add(d):
            if d is not None and deps.get(d[0], 0) < d[1]:
                deps[d[0]] = d[1]

        for s in reads:
            add(s.w)
            for c in s.conf:
                add(c.w)
        for s in writes:
            add(s.w)
            for kv in s.r.items():
                add(kv)
            for c in s.conf:
                add(c.w)
                for kv in c.r.items():
                    add(kv)
        waits = []
        obs = self.observed[eng]
        for k, v in deps.items():
            if k == ("eng", eng) and not (self.same and eng != "pe"):
                continue
            if obs.get(k, 0) >= v:
                continue
            obs[k] = v
            waits.append((k, v))
        if dsem is not None:
            self.cnt[dsem] += 16 * ndma
            me = (dsem, self.cnt[dsem])
        else:
            self.cnt[eng] += 1
            me = (("eng", eng), self.cnt[eng])
        self.q[eng].append((waits, fn, me[0]))
        for s in reads:
            if s.r.get(me[0], 0) < me[1]:
                s.r[me[0]] = me[1]
        for s in writes:
            s.w = me
            s.r = {}
        return me

    def emit(self, final_waits):
        nc = self.nc
        with nc.Block() as block:
            def mk(engname):
                def body(e):
                    for waits, fn, key in self.q[engname]:
                        for k, v in waits:
                            e.wait_ge(self.semobj[k], v)
                        insts = fn(e)
                        if not isinstance(insts, (list, tuple)):
                            insts = [insts]
                        if key[0] == "dma":
                            for i in insts:
                                i.then_inc(self.semobj[key], 16)
                        else:
                            insts[-1].then_inc(self.semobj[key], 1)
                    if engname == "sp":
                        for k, v in final_waits:
                            e.wait_ge(self.semobj[k], v)
                return body
            block.tensor(mk("pe"))
            block.scalar(mk("act"))
            block.vector(mk("dve"))
            block.gpsimd(mk("pool"))
            block.sync(mk("sp"))


def build_nc(ntiles=2 * NTILE_SEQ, debug=False, upto=9):
    nc = bass.Bass("TRN2", target_bir_lowering=False)
    dram = nc.dram_tensor
    NTOK = 2 * SEQ
    x_d = dram("x", [NTOK, D], F32, kind="ExternalInput").ap()
    pos_d = dram("pos", [2, SEQ], I32, kind="ExternalInput").ap()
    win_d = dram("w_in", [D, 3328], F32, kind="ExternalInput").ap()
    wpw_d = dram("w_pw2", [1024, 1024], F32, kind="ExternalInput").ap()
    wout_d = dram("w_out", [D, D], F32, kind="ExternalInput").ap()
    wgu_d = dram("w_gu", [D, 2 * DFF], F32, kind="ExternalInput").ap()
    wdn_d = dram("w_down", [DFF, D], F32, kind="ExternalInput").ap()
    pcol_d = dram("pcol", [128, NPC], F32, kind="ExternalInput").ap()
    prow_d = dram("prow", [128, 144], F32, kind="ExternalInput").ap()
    cst_d = dram("cst", [128, 512], F32, kind="ExternalInput").ap()
    out_d = dram("out", [NTOK, D], F32, kind="ExternalOutput").ap()
    winb = dram("w_in_b", [D, 3328], BF16, kind="Internal").ap()
    wpwb = dram("w_pw2_b", [1024, 1024], BF16, kind="Internal").ap()
    woutb = dram("w_out_b", [D, D], BF16, kind="Internal").ap()
    wgub = dram("w_gu_b", [D, 2 * DFF], BF16, kind="Internal").ap()
    wdnb = dram("w_down_b", [DFF, D], BF16, kind="Internal").ap()
    dbg_out = {}
    if debug:
        for nm, shp in (("d_mix", [128, 16 * 512]), ("d_x1", [128, 16 * 512]), ("d_q", [128, 8 * 512]),
                        ("d_k", [128, 512]), ("d_u", [128, 8 * 542]), ("d_un", [128, 8 * 512])):
            dbg_out[nm] = dram(nm, shp, F32 if nm == "d_x1" else BF16, kind="ExternalOutput").ap()

    with ExitStack() as st:
        S = Sched(nc, st)
        sb = lambda name, shape, dt: st.enter_context(nc.sbuf_tensor("sb_" + name, shape, dt))
        R = sb("R", [128, 16, T], F32)
        xT = sb("xT", [128, 16, T], BF16)
        U = sb("U", [128, 24320], BF16)
        hT = U[:, 0:NF * T].rearrange("p (f t) -> p f t", t=T)
        P = U[:, 0:4096].rearrange("p (s k) -> p s k", k=256)
        PTv = U[:, 4096:8192].rearrange("p (a c q) -> p a c q", c=8, q=128)
        Dg = [U[:, 8192 + i * 3968: 8192 + (i + 1) * 3968].rearrange("p (j m) -> p j m", m=128) for i in range(2)]
        mixT = U[:, 16128:16128 + 8192].rearrange("p (c t) -> p c t", t=T)
        wbuf = [sb("wbuf%d" % i, [128, 16, 512], BF16) for i in range(2)]
        QX = sb("QX", [128, 2048], F32)
        UX = sb("UX", [128, 2048], F32)
        qT = QX[:].bitcast(BF16)[:, 0:8 * T].rearrange("p (c t) -> p c t", t=T)
        un = UX[:].bitcast(BF16)[:, 0:8 * T].rearrange("p (c t) -> p c t", t=T)
        xs = [QX[:], UX[:], U[:, 0:4096].bitcast(F32), U[:, 4096:8192].bitcast(F32)]
        YY = sb("YY", [128, 8, T], F32)
        yo = [YY[:, 0:4, :].rearrange("p a t -> p (a t)"), YY[:, 4:8, :].rearrange("p a t -> p (a t)")]
        tmp = [sb("tmp%d" % i, [128, T], F32) for i in range(4)]
        tmpb = [t[:].bitcast(BF16) for t in tmp]
        qsb = [sb("qsb%d" % i, [128, T], BF16) for i in range(2)]
        kT = sb("kT", [128, 128 + T], BF16)
        kTp = sb("kTp", [128, 2, 128 + T], BF16)
        Vp = sb("Vp", [128, 5, 2, 128], BF16)
        cosT = sb("cosT", [128, T], F32)
        sinT = sb("sinT", [128, T], F32)
        uT = sb("uT", [128, 8, HALO + T], BF16)
        stm = sb("stm", [128, T], F32)
        stv = sb("stv", [128, T], F32)
        ident32 = sb("ident32", [128, 128], F32)
        onesb = sb("onesb", [128, 128], BF16)
        identb = sb("identb", [128, 128], BF16)
        rpermb = sb("rpermb", [128, 128], BF16)
        maskb = sb("maskb", [128, 256], BF16)
        negmb = sb("negmb", [128, 512], BF16)
        pcol = sb("pcol", [128, NPC], F32)
        prow = sb("prow", [128, 144], F32)
        ag = sb("ag", [128, 32], F32)
        sm = sb("sm", [128, 6, 16], F32)
        PS = st.enter_context(nc.psum_tensor("PS", [128, 8, 512], F32))

        mk = lambda n, k: [Slot("%s%d" % (n, i)) for i in range(k)]
        s_R = mk("R", 16); s_xT = mk("xT", 16); s_h = mk("h", NF); s_P = mk("P", 4); s_PT = mk("PT", 4)
        s_D = mk("D", 2); s_mix = mk("mix", 16); s_w = mk("w", 2); s_q = mk("q", 8); s_un = mk("un", 8)
        s_xs = mk("xs", 4); s_Y = mk("Y", 8); s_yo = mk("yo", 2); s_tmp = mk("tmp", 4); s_qsb = mk("qsb", 2)
        s_kT = Slot("kT"); s_V = Slot("V"); s_cos = Slot("cos"); s_sin = Slot("sin"); s_u = mk("u", 8)
        s_uh = Slot("uhalo"); s_stm = Slot("stm"); s_stv = Slot("stv"); s_c = Slot("const"); s_sm = mk("sm", 4); s_B = [Slot("B%d" % i, excl=True) for i in range(8)]
        s_cast = {}
        alias(s_h, s_P + s_PT + s_D + s_mix)
        alias([s_xs[0]], s_q)
        alias([s_xs[1]], s_un)
        alias(s_xs[2:4], s_h + s_P + s_PT)
        alias([s_yo[0]], s_Y[0:4])
        alias([s_yo[1]], s_Y[4:8])

        def cast(name, dst, src):
            s_cast[name] = Slot("cast_" + name)
            if os.environ.get("K_NOCAST"):
                return
            ds = S.dma_sem("c_" + name)
            S.op("pool", lambda e: e.dma_start(out=dst, in_=src), writes=[s_cast[name]], dsem=ds)

        cast("in", winb.rearrange("r (a b) -> r a b", b=832), win_d.rearrange("r (a b) -> r a b", b=832))
        d_cst = S.dma_sem("cst")
        ld_c = [None]

        def load_consts(e):
            return [e.dma_start(out=pcol[:], in_=pcol_d), e.dma_start(out=prow[:], in_=prow_d),
                    e.dma_start(out=tmp[0][:], in_=cst_d)]
        S.op("sp", load_consts, writes=[s_c, s_tmp[0]], dsem=d_cst, ndma=3)

        def mkconst(e):
            e.tensor_copy(out=ident32[:], in_=tmp[0][:, 0:128])
            e.tensor_copy(out=identb[:], in_=tmp[0][:, 0:128])
            e.tensor_copy(out=rpermb[:], in_=tmp[0][:, 128:256])
            e.tensor_copy(out=maskb[:], in_=tmp[0][:, 256:512])
            e.tensor_scalar(out=negmb[:, 0:256], in0=tmp[0][:, 256:512], scalar1=30000.0, scalar2=-30000.0, op0=ALU.mult, op1=ALU.add)
            e.tensor_scalar(out=negmb[:, 256:512], in0=tmp[0][:, 256:512], scalar1=30000.0, scalar2=-30000.0, op0=ALU.mult, op1=ALU.add)
            e.memset(onesb[:], 1.0)
            e.memset(Vp[:], 0.0)
            e.memset(kTp[:], 0.0)
            e.tensor_scalar(out=ag[:, 0:16], in0=pcol[:, PC_G1:PC_G1 + 16], scalar1=ALPHA, scalar2=None, op0=ALU.mult)
            return e.tensor_scalar(out=ag[:, 16:32], in0=pcol[:, PC_B1:PC_B1 + 16], scalar1=ALPHA, scalar2=None, op0=ALU.mult)
        S.op("dve", mkconst, reads=[s_tmp[0]], writes=[s_c, s_V])

        cast("pw", wpwb, wpw_d)
        cast("out", woutb, wout_d)
        GU_PIECES = [(0, 6), (6, 12), (12, 17), (17, 22)]
        for i, (a, b) in enumerate(GU_PIECES):
            cast("gu%d" % i, wgub[:, a * 512:b * 512].rearrange("r (a b) -> r a b", b=512),
                 wgu_d[:, a * 512:b * 512].rearrange("r (a b) -> r a b", b=512))
        for i in range(2):
            cast("dn%d" % i, wdnb[:, i * 1024:(i + 1) * 1024], wdn_d[:, i * 1024:(i + 1) * 1024])

        groups = []
        for _ in range(ntiles):
            for g in (6, 0, 1, 2, 3, 4, 5):
                nw = 512 if g < 6 else 256
                groups.append((winb[:, g * 512:g * 512 + nw].rearrange("(kc p) n -> p kc n", p=128), "in", 16, nw))
            for g in range(2):
                groups.append((wpwb[:, g * 512:(g + 1) * 512].rearrange("(kc p) n -> p kc n", p=128), "pw", 8, 512))
            for g in range(4):
                groups.append((woutb[:, g * 512:(g + 1) * 512].rearrange("(kc p) n -> p kc n", p=128), "out", 16, 512))
            for g in range(22):
                pi = [i for i, (a, b) in enumerate(GU_PIECES) if a <= g < b][0]
                groups.append((wgub[:, g * 512:(g + 1) * 512].rearrange("(kc p) n -> p kc n", p=128), "gu%d" % pi, 16, 512))
            for n in range(4):
                for kg in range(4):
                    groups.append((wdnb[kg * 1408:(kg + 1) * 1408, n * 512:(n + 1) * 512].rearrange("(kc p) n -> p kc n", p=128),
                                   "dn%d" % (n // 2), 11, 512))
        d_w = [S.dma_sem("w%d" % i) for i in range(2)]
        wstate = {"load": 0, "use": 0}

        def w_acquire():
            i = wstate["use"]
            while wstate["load"] < min(len(groups), i + 2):
                li = wstate["load"]
                src, cname, kc, nw = groups[li]
                bi = li % 2
                S.op("sp", lambda e, src=src, kc=kc, nw=nw, bi=bi: e.dma_start(out=wbuf[bi][:, 0:kc, 0:nw], in_=src),
                     reads=[s_cast[cname]], writes=[s_w[bi]], dsem=d_w[bi])
                wstate["load"] += 1
            wstate["use"] += 1
            return wbuf[i % 2], s_w[i % 2]

        d_xs = [S.dma_sem("xs%d" % i) for i in range(4)]
        d_yo = [S.dma_sem("yo%d" % i) for i in range(8)]
        d_pos = S.dma_sem("pos")
        dbg_sems = []

        def pcs(col):
            return pcol[:, col:col + 1]

        bank_rr = {"i": 0}

        def ln_finish(inv_n, b1=2, b2=3):
            S.op("dve", lambda e: e.tensor_scalar(out=stm[:], in0=PS[:, b1, :], scalar1=inv_n, scalar2=None, op0=ALU.mult),
                 reads=[s_B[b1]], writes=[s_stm])
            S.op("dve", lambda e: e.tensor_tensor(out=stv[:], in0=stm[:], in1=stm[:], op=ALU.mult), reads=[s_stm], writes=[s_stv])
            S.op("dve", lambda e: e.scalar_tensor_tensor(out=stv[:], in0=PS[:, b2, :], scalar=inv_n, in1=stv[:], op0=ALU.mult, op1=ALU.subtract),
                 reads=[s_B[b2]], writes=[s_stv])
            S.op("act", lambda e: e.activation(out=stv[:], in_=stv[:], func=AF.Ln, bias=EPS_AP[:, 0:1], scale=1.0),
                 reads=[s_c], writes=[s_stv])
            S.op("act", lambda e: e.activation(out=PS[:, b2, :], in_=stv[:], func=AF.Exp, scale=-0.5),
                 reads=[s_stv], writes=[s_B[b2]])
            S.op("dve", lambda e: e.scalar_tensor_tensor(out=PS[:, b1, :], in0=stm[:], scalar=-1.0, in1=PS[:, b2, :], op0=ALU.mult, op1=ALU.mult),
                 reads=[s_stm, s_B[b2]], writes=[s_B[b1]])
            S.op("dve", lambda e: e.tensor_copy(out=stm[:], in_=PS[:, b1, :]), reads=[s_B[b1]], writes=[s_stm])

        def ln_normalize(ap, slot, idx, b1=2, b2=3):
            S.op("dve", lambda e: e.tensor_tensor(out=ap, in0=ap, in1=PS[:, b2, :], op=ALU.mult), reads=[s_B[b2]], writes=[slot])
            if idx % 2 == 0:
                S.op("dve", lambda e: e.tensor_tensor(out=ap, in0=ap, in1=PS[:, b1, :], op=ALU.add), reads=[s_B[b1]], writes=[slot])
            else:
                S.op("pool", lambda e: e.tensor_tensor(out=ap, in0=ap, in1=stm[:], op=ALU.add), reads=[s_stm], writes=[slot])

        EPS_AP = sb("epsap", [128, 1], F32)
        S.op("dve", lambda e: e.memset(EPS_AP[:], EPS), writes=[s_c])

        def stats_mm(src_ap, src_slot, bank, first, last):
            S.op("pe", lambda e: e.matmul(PS[:, bank, :], lhsT=onesb[:], rhs=src_ap, start=first, stop=last),
                 reads=[src_slot, s_c], writes=[s_B[bank]])

        def reduce_angle(a, sa, k, sk):
            S.op("dve", lambda e: e.tensor_scalar(out=k, in0=a, scalar1=float(1.0 / TWO_PI), scalar2=MAGIC, op0=ALU.mult, op1=ALU.add),
                 reads=[sa], writes=[sk])
            S.op("dve", lambda e: e.tensor_scalar(out=k, in0=k, scalar1=-MAGIC, scalar2=None, op0=ALU.add), writes=[sk])
            S.op("dve", lambda e: e.scalar_tensor_tensor(out=a, in0=k, scalar=-C1, in1=a, op0=ALU.mult, op1=ALU.add), reads=[sk], writes=[sa])
            S.op("dve", lambda e: e.scalar_tensor_tensor(out=a, in0=k, scalar=-C2, in1=a, op0=ALU.mult, op1=ALU.add), reads=[sk], writes=[sa])
            S.op("dve", lambda e: e.tensor_scalar(out=a, in0=a, scalar1=PI_LO, scalar2=-PI_LO, op0=ALU.min, op1=ALU.max), writes=[sa])

        def dbg_dump(name, ap, slots):
            if not debug:
                return
            ds = S.dma_sem("dbg_" + name)
            dbg_sems.append(ds)
            S.op("sp", lambda e: e.dma_start(out=dbg_out[name], in_=ap), reads=slots, writes=[], dsem=ds)

        def load_x_block(tok0, b, bi):
            S.op("sp", lambda e: e.dma_start(out=xs[bi], in_=x_d[tok0 + b * 128: tok0 + (b + 1) * 128, :]),
                 writes=[s_xs[bi]], dsem=d_xs[bi])

        def gen_tables(tix):
            sq_, tt_ = tix // NTILE_SEQ, tix % NTILE_SEQ
            S.op("sp", lambda e: e.dma_start(out=sinT[:].bitcast(I32),
                                             in_=pos_d[sq_:sq_ + 1, tt_ * T:(tt_ + 1) * T].partition_broadcast(128)),
                 writes=[s_sin], dsem=d_pos)
            S.op("dve", lambda e: e.tensor_copy(out=cosT[:], in_=sinT[:].bitcast(I32)), reads=[s_sin], writes=[s_cos])
            S.op("dve", lambda e: e.tensor_scalar(out=cosT[:], in0=cosT[:], scalar1=pcs(PC_INVF), scalar2=None, op0=ALU.mult),
                 reads=[s_c], writes=[s_cos])
            S.op("dve", lambda e: e.tensor_scalar(out=sinT[:], in0=cosT[:], scalar1=float(np.pi / 2), scalar2=None, op0=ALU.add),
                 reads=[s_cos], writes=[s_sin])
            reduce_angle(cosT[:], s_cos, tmp[2][:], s_tmp[2])
            reduce_angle(sinT[:], s_sin, tmp[2][:], s_tmp[2])
            S.op("act", lambda e: e.activation(out=tmp[3][:], in_=cosT[:], func=AF.Sin, scale=pcs(PC_SIGN)),
                 reads=[s_cos, s_c], writes=[s_tmp[3]])
            S.op("act", lambda e: e.activation(out=cosT[:], in_=sinT[:], func=AF.Sin), reads=[s_sin], writes=[s_cos])
            S.op("act", lambda e: e.activation(out=sinT[:], in_=tmp[3][:], func=AF.Identity), reads=[s_tmp[3]], writes=[s_sin])

        pend_stats = []

        def flush_stats():
            while pend_stats:
                pend_stats.pop(0)()

        def resid_stats(dc, bank, b1=2, b2=3, bias_col=None):
            if bias_col is None:
                S.op("dve", lambda e: e.tensor_tensor(out=R[:, dc, :], in0=PS[:, bank, :], in1=R[:, dc, :], op=ALU.add),
                     reads=[s_B[bank]], writes=[s_R[dc]])
            else:
                S.op("dve", lambda e: e.scalar_tensor_tensor(out=R[:, dc, :], in0=PS[:, bank, :], scalar=pcs(bias_col), in1=R[:, dc, :],
                                                             op0=ALU.add, op1=ALU.add), reads=[s_B[bank], s_c], writes=[s_R[dc]])
            sq = dc % 4
            while len(pend_stats) >= 4:
                pend_stats.pop(0)()

            def sqy(e):
                e.activation(out=tmpb[sq][:, 0:T], in_=R[:, dc, :], func=AF.Square)
                return e.activation(out=tmpb[sq][:, T:2 * T], in_=R[:, dc, :], func=AF.Identity)
            S.op("act", sqy, reads=[s_R[dc]], writes=[s_tmp[sq]])

            def later():
                stats_mm(tmpb[sq][:, T:2 * T], s_tmp[sq], b1, dc == 0, dc == 15)
                stats_mm(tmpb[sq][:, 0:T], s_tmp[sq], b2, dc == 0, dc == 15)
            pend_stats.append(later)

        def kouter_group(wb, wslot, src, src_slots, K, banks):
            for kc in range(K):
                def mm(e, kc=kc):
                    r = None
                    for cc in range(4):
                        r = e.matmul(PS[:, banks[cc], :], lhsT=wb[:, kc, cc * 128:(cc + 1) * 128], rhs=src[:, kc, :],
                                     start=(kc == 0), stop=(kc == K - 1))
                    return r
                S.op("pe", mm, reads=[wslot, src_slots[kc]], writes=[s_B[b] for b in banks])

        for ti in range(ntiles):
            seq = ti // NTILE_SEQ
            tt = ti % NTILE_SEQ
            tok0 = seq * SEQ + tt * T
            first = tt == 0
            dbg_here = debug and ti == 0

            if upto < -2:
                continue
            if ti == 0:
                gen_tables(0)

            if upto < -1:
                continue
            for b in range(4):
                bi = b
                if ti == 0:
                    load_x_block(tok0, b, bi)
                for g in range(4):
                    bank = g

                    def tr(e, g=g, bi=bi, bank=bank):
                        r = None
                        for i in range(4):
                            kc = 4 * g + i
                            r = e.transpose(out=PS[:, bank, i * 128:(i + 1) * 128], in_=xs[bi][:, kc * 128:(kc + 1) * 128],
                                            identity=ident32[:])
                        return r
                    S.op("pe", tr, reads=[s_xs[bi], s_c], writes=[s_B[bank]])

                    S.op("act", lambda e, g=g, b=b, bank=bank: e.activation(
                        out=R[:, 4 * g:4 * g + 4, b * 128:(b + 1) * 128], in_=PS[:, bank, :].rearrange("p (i t) -> p i t", t=128),
                        func=AF.Identity, scale=ALPHA), reads=[s_B[bank]], writes=s_R[4 * g:4 * g + 4])
                    S.op("dve", lambda e, g=g, b=b, bank=bank: e.tensor_copy(
                        out=xT[:, 4 * g:4 * g + 4, b * 128:(b + 1) * 128],
                        in_=PS[:, bank, :].rearrange("p (i t) -> p i t", t=128)),
                        reads=[s_B[bank]], writes=s_xT[4 * g:4 * g + 4])

            if upto < 0:
                continue
            if first:
                S.op("dve", lambda e: e.memset(uT[:, :, 0:HALO], 0.0), writes=[s_uh])
            acc_banks = [2, 3, 4]
            rot_banks = [5, 6]
            ctr = {"acc": 0, "rot": 0, "tmp": 0}

            def inproj_mm(wb, wslot, cc, bank):
                def mm(e, wb=wb, cc=cc, bank=bank):
                    r = None
                    for kc in range(16):
                        r = e.matmul(PS[:, bank, :], lhsT=wb[:, kc, cc * 128:(cc + 1) * 128], rhs=xT[:, kc, :],
                                     start=(kc == 0), stop=(kc == 15))
                    return r
                S.op("pe", mm, reads=[wslot] + s_xT, writes=[s_B[bank]])

            pend_rot = []

            def flush_rot():
                while pend_rot:
                    pend_rot.pop(0)()

            def qk_chunk(wb, wslot, cc, ch):
                bank = acc_banks[ctr["acc"] % 3]
                ctr["acc"] += 1
                inproj_mm(wb, wslot, cc, bank)
                qi = ctr["rot"] % 2
                rb = rot_banks[ctr["rot"] % 2]
                ctr["rot"] += 1
                t1 = 2 + (ctr["tmp"] % 2)
                ctr["tmp"] += 1
                t2 = 0 if t1 == 2 else 1
                flush_rot()
                S.op("act", lambda e: e.activation(out=qsb[qi][:], in_=PS[:, bank, :], func=AF.Identity, bias=pcs(PC_BIN + ch), scale=1.0),
                     reads=[s_B[bank], s_c], writes=[s_qsb[qi]])
                if ch < 8:
                    dst, dslot = qT[:, ch, :], s_q[ch]
                else:
                    dst, dslot = kT[:, 128:128 + T], s_kT

                def later():
                    S.op("pe", lambda e: e.matmul(PS[:, rb, :], lhsT=rpermb[:], rhs=qsb[qi][:], start=True, stop=True),
                         reads=[s_qsb[qi], s_c], writes=[s_B[rb]])
                    S.op("pool", lambda e: e.tensor_tensor(out=tmp[t1][:], in0=qsb[qi][:], in1=cosT[:], op=ALU.mult),
                         reads=[s_qsb[qi], s_cos], writes=[s_tmp[t1]])
                    S.op("dve", lambda e: e.tensor_tensor(out=tmp[t2][:], in0=PS[:, rb, :], in1=sinT[:], op=ALU.mult),
                         reads=[s_B[rb], s_sin], writes=[s_tmp[t2]])
                    if ch < 8:
                        S.op("dve", lambda e: e.tensor_tensor(out=dst, in0=tmp[t1][:], in1=tmp[t2][:], op=ALU.add),
                             reads=[s_tmp[t1], s_tmp[t2]], writes=[dslot])
                    else:
                        def kadd(e):
                            e.tensor_tensor(out=dst, in0=tmp[t1][:], in1=tmp[t2][:], op=ALU.add)
                            e.tensor_tensor(out=kTp[0:64, 0, 128:128 + T], in0=tmp[t1][0:64, :], in1=tmp[t2][0:64, :], op=ALU.add)
                            return e.tensor_tensor(out=kTp[64:128, 1, 128:128 + T], in0=tmp[t1][64:128, :], in1=tmp[t2][64:128, :], op=ALU.add)
                        S.op("dve", kadd, reads=[s_tmp[t1], s_tmp[t2]], writes=[dslot])
                pend_rot.append(later)

            wb, wslot = w_acquire()
            qk_chunk(wb, wslot, 0, 24)
            def mmv(e, wb=wb):
                r = None
                for b4 in range(4):
                    for kc in range(16):
                        r = e.matmul(PS[:, 7, b4 * 128:(b4 + 1) * 128], lhsT=xT[:, kc, b4 * 128:(b4 + 1) * 128], rhs=wb[:, kc, 128:256],
                                     start=(kc == 0), stop=(kc == 15))
                return r
            S.op("pe", mmv, reads=[wslot] + s_xT, writes=[s_B[7]])
            flush_rot()

            def evv(e):
                pv4 = PS[:, 7, :].rearrange("p (b n) -> p b n", n=128)
                e.tensor_tensor(out=Vp[:, 1:5, 0, 0:64], in0=pv4[:, :, 0:64],
                                in1=prow[:, 0:64].unsqueeze(1).broadcast_to([128, 4, 64]), op=ALU.add)
                return e.tensor_tensor(out=Vp[:, 1:5, 1, 64:128], in0=pv4[:, :, 64:128],
                                       in1=prow[:, 64:128].unsqueeze(1).broadcast_to([128, 4, 64]), op=ALU.add)
            S.op("dve", evv, reads=[s_B[7], s_c], writes=[s_V])
            for g in range(2):
                wb, wslot = w_acquire()
                for cc in range(4):
                    qk_chunk(wb, wslot, cc, 4 * g + cc)
            flush_rot()
            if dbg_here:
                dbg_dump("d_q", qT.rearrange("p c t -> p (c t)"), s_q)
                dbg_dump("d_k", kT[:, 128:128 + T], [s_kT])

            wstate2 = {}

            def pair_item(c):
                if c % 2 == 0:
                    wstate2["w"] = w_acquire()
                wb, wslot = wstate2["w"]
                cc0 = (c % 2) * 2
                sgi = c % 2
                inproj_mm(wb, wslot, cc0, 4)
                S.op("act", lambda e: e.activation(out=tmp[sgi][:], in_=PS[:, 4, :], func=AF.Sigmoid, bias=pcs(PC_BIN + 8 + 2 * c), scale=1.0),
                     reads=[s_B[4], s_c], writes=[s_tmp[sgi]])
                inproj_mm(wb, wslot, cc0 + 1, 5)
                S.op("dve", lambda e: e.scalar_tensor_tensor(out=uT[:, c, HALO:HALO + T], in0=PS[:, 5, :], scalar=pcs(PC_BIN + 9 + 2 * c),
                                                             in1=tmp[sgi][:], op0=ALU.add, op1=ALU.mult),
                     reads=[s_B[5], s_tmp[sgi], s_c], writes=[s_u[c]])

            def conv_prep(c):
                di = c % 2

                def dgen(e):
                    r = None
                    for jj in range(KW):
                        r = e.tensor_scalar(out=Dg[di][:, jj, :], in0=identb[:], scalar1=pcs(PC_WDW + c * KW + jj), scalar2=None, op0=ALU.mult)
                    return r
                S.op("dve", dgen, reads=[s_c], writes=[s_D[di]])

            def conv_item(c):
                di = c % 2
                bank = 4 + c % 2
                sq = 2 + c % 2

                def cv(e):
                    r = None
                    for jj in range(KW):
                        r = e.matmul(PS[:, bank, :], lhsT=Dg[di][:, jj, :], rhs=uT[:, c, jj:jj + T], start=(jj == 0), stop=(jj == KW - 1))
                    return r
                S.op("pe", cv, reads=[s_D[di], s_u[c], s_uh], writes=[s_B[bank]])
                S.op("act", lambda e: e.activation(out=YY[:, c, :], in_=PS[:, bank, :], func=AF.Identity, bias=pcs(PC_BDW + c), scale=1.0),
                     reads=[s_B[bank], s_c], writes=[s_Y[c]])

                def sqy(e):
                    e.activation(out=tmpb[sq][:, 0:T], in_=PS[:, bank, :], func=AF.Square, bias=pcs(PC_BDW + c), scale=1.0)
                    return e.activation(out=tmpb[sq][:, T:2 * T], in_=PS[:, bank, :], func=AF.Identity, bias=pcs(PC_BDW + c), scale=1.0)
                S.op("act", sqy, reads=[s_B[bank], s_c], writes=[s_tmp[sq]])
                def later():
                    stats_mm(tmpb[sq][:, T:2 * T], s_tmp[sq], 6, c == 0, c == 7)
                    stats_mm(tmpb[sq][:, 0:T], s_tmp[sq], 7, c == 0, c == 7)
                pend_stats.append(later)

            items = []
            for c in range(8):
                items.append(("pair", c))
                if c >= 1:
                    items.append(("conv", c - 1))
            items.append(("conv", 7))
            items.append(("cln", 0))
            for oc in range(8):
                items.append(("pw", oc))

            def cln_item():
                flush_stats()
                ln_finish(1.0 / 1024, 6, 7)
                for c in range(8):
                    ln_normalize(YY[:, c, :], s_Y[c], c, 6, 7)
                    S.op("act", lambda e, c=c: e.activation(out=un[:, c, :], in_=YY[:, c, :], func=AF.Silu, bias=pcs(PC_CLB + c),
                                                            scale=pcs(PC_CLG + c)),
                         reads=[s_Y[c], s_c], writes=[s_un[c]])

            def pw_item(oc):
                if oc % 4 == 0:
                    wstate2["pw"] = w_acquire()
                wb, wslot = wstate2["pw"]
                cc = oc % 4
                bank = 4 + oc % 2

                def mm(e):
                    r = None
                    for kc in range(8):
                        r = e.matmul(PS[:, bank, :], lhsT=wb[:, kc, cc * 128:(cc + 1) * 128], rhs=un[:, kc, :], start=(kc == 0), stop=(kc == 7))
                    return r
                S.op("pe", mm, reads=[wslot] + s_un, writes=[s_B[bank]])
                S.op("act", lambda e: e.activation(out=mixT[:, 8 + oc, :], in_=PS[:, bank, :], func=AF.Identity,
                                                   bias=pcs(PC_BPW + oc), scale=1.0),
                     reads=[s_B[bank], s_c], writes=[s_mix[8 + oc]])

            def run_item():
                if not items:
                    return
                kind, c = items.pop(0)
                if kind == "pair":
                    pair_item(c)
                    flush_stats()
                elif kind == "cln":
                    cln_item()
                elif kind == "pw":
                    pw_item(c)
                else:
                    todo = list(pend_stats)
                    del pend_stats[:]
                    conv_item(c)
                    for fn in todo:
                        fn()
                for k2, c2 in items[:2]:
                    if k2 == "conv" and c2 not in prepped:
                        conv_prep(c2)
                        prepped.add(c2)
                        break
            prepped = set()
            conv_prep(0)
            prepped.add(0)

            pend_T = []
            pend_V = []
            rmax, mmx, negm, Ex, sums, rden = [sm[:, i, :] for i in range(6)]
            sinkrow = prow[:, 128:144]
            for j in range(4):
                nb = tt * 4 + j
                k0 = 0 if nb > 0 else 128
                L = 256 - k0
                kb0 = k0 // 128
                for sg in range(4):
                    bA = 0
                    ptb = 2
                    psc = PS[:, bA:bA + 2, :].rearrange("p b (s k) -> p (b s) k", k=256)
                    sl = slice(4 * sg, 4 * sg + 4)

                    def sc(e, sg=sg, j=j, k0=k0, bA=bA):
                        r = None
                        for rr in range(2):
                            e.matmul(PS[:, bA + rr, :], lhsT=identb[:], rhs=negmb[:], start=True, stop=False)
                        for rr in range(2):
                            for ccx in range(2):
                                c = 2 * sg + ccx
                                col = ccx * 256
                                r = e.matmul(PS[:, bA + rr, col + k0:col + 256], lhsT=qT[:, c, j * 128:(j + 1) * 128],
                                             rhs=kTp[:, rr, j * 128 + k0:j * 128 + 256], start=False, stop=(ccx == 1))
                        return r
                    S.op("pe", sc, reads=[s_q[2 * sg], s_q[2 * sg + 1], s_kT, s_c], writes=[s_B[bA], s_B[bA + 1]])
                    while pend_T:
                        pend_T.pop(0)()

                    S.op("dve", lambda e, psc=psc, sl=sl, k0=k0: e.tensor_reduce(out=rmax[:, sl], in_=psc[:, :, k0:256], axis=AX.X, op=ALU.max),
                         reads=[s_B[bA], s_B[bA + 1]], writes=[s_sm[sg]])
                    S.op("dve", lambda e, sl=sl: e.scalar_tensor_tensor(out=mmx[:, sl], in0=rmax[:, sl], scalar=0.125, in1=sinkrow[:, sl],
                                                                          op0=ALU.mult, op1=ALU.max), reads=[s_c], writes=[s_sm[sg]])
                    S.op("dve", lambda e, sl=sl: e.tensor_scalar(out=negm[:, sl], in0=mmx[:, sl], scalar1=-1.0, scalar2=None, op0=ALU.mult),
                         writes=[s_sm[sg]])
                    S.op("dve", lambda e, sl=sl: e.tensor_tensor(out=Ex[:, sl], in0=negm[:, sl], in1=sinkrow[:, sl], op=ALU.add),
                         reads=[s_c], writes=[s_sm[sg]])

                    def ex(e, psc=psc, sg=sg, k0=k0, sl=sl):
                        for n in range(4):
                            s = 4 * sg + n
                            e.activation(out=P[:, s, k0:256], in_=psc[:, n, k0:256], func=AF.Exp, bias=negm[:, s:s + 1], scale=0.125,
                                         accum_out=sums[:, s:s + 1])
                        return e.activation(out=Ex[:, sl], in_=Ex[:, sl], func=AF.Exp)
                    S.op("act", ex, reads=[s_B[bA], s_B[bA + 1], s_sm[sg]], writes=[s_P[sg], s_sm[sg]])

                    S.op("dve", lambda e, sl=sl: e.tensor_tensor(out=sums[:, sl], in0=sums[:, sl], in1=Ex[:, sl], op=ALU.add), writes=[s_sm[sg]])
                    S.op("dve", lambda e, sl=sl: e.reciprocal(out=rden[:, sl], in_=sums[:, sl]), writes=[s_sm[sg]])
                    S.op("dve", lambda e, sl=sl, k0=k0, L=L: e.tensor_tensor(
                        out=P[:, sl, k0:256], in0=P[:, sl, k0:256],
                        in1=rden[:, sl].unsqueeze(2).broadcast_to([128, 4, L]), op=ALU.mult), reads=[s_sm[sg]], writes=[s_P[sg]])

                    if items and items[0][0] in ("pair", "conv"):
                        run_item()
                    while pend_V:
                        pend_V.pop(0)()

                    ptps = PS[:, ptb, :].bitcast(BF16)

                    def tpart(sg=sg, kb0=kb0, ptps=ptps, j=j):
                        def trp(e, sg=sg, kb0=kb0, ptps=ptps):
                            r = None
                            for kb in range(kb0, 2):
                                for n in range(4):
                                    idx = kb * 4 + n
                                    r = e.transpose(out=ptps[:, idx * 128:(idx + 1) * 128], in_=P[:, 4 * sg + n, kb * 128:(kb + 1) * 128],
                                                    identity=identb[:])
                            return r
                        S.op("pe", trp, reads=[s_P[sg], s_c], writes=[s_B[ptb]])
                        S.op("act", lambda e, sg=sg, kb0=kb0, ptps=ptps: e.activation(
                            out=PTv[:, 2 * kb0:4, 2 * sg:2 * sg + 2, :],
                            in_=ptps[:, kb0 * 512:1024].rearrange("p (a c q) -> p a c q", c=2, q=128), func=AF.Identity),
                            reads=[s_B[ptb]], writes=[s_PT[sg]])
                    def vpart(sg=sg, kb0=kb0, j=j):
                        if sg % 2 == 1:
                            cg = sg // 2
                            ob = 3

                            def pv(e, cg=cg, ob=ob, j=j, kb0=kb0):
                                r = None
                                n = 0
                                tot = (2 - kb0) * 2
                                for kb in range(kb0, 2):
                                    for rr in range(2):
                                        r = e.matmul(PS[:, ob, :], lhsT=Vp[:, j + kb, rr, :], rhs=PTv[:, kb * 2 + rr, 4 * cg:4 * cg + 4, :],
                                                     start=(n == 0), stop=(n == tot - 1))
                                        n += 1
                                return r
                            S.op("pe", pv, reads=[s_V, s_PT[sg - 1], s_PT[sg]], writes=[s_B[ob]])
                            S.op("dve", lambda e, cg=cg, ob=ob, j=j: e.tensor_copy(
                                out=mixT[:, 4 * cg:4 * cg + 4, j * 128:(j + 1) * 128], in_=PS[:, ob, :].rearrange("p (c q) -> p c q", q=128)),
                                reads=[s_B[ob]], writes=s_mix[4 * cg:4 * cg + 4])
                    pend_T.append(tpart)
                    pend_T.append(lambda vp=vpart: pend_V.append(vp))
            while pend_T:
                pend_T.pop(0)()
            while pend_V:
                pend_V.pop(0)()
            while items:
                run_item()
            flush_stats()
            if dbg_here:
                dbg_dump("d_u", uT[:].rearrange("p c t -> p (c t)"), s_u + [s_uh])

            if tt < NTILE_SEQ - 1:
                def carry(e):
                    e.tensor_copy(out=uT[:, :, 0:HALO], in_=uT[:, :, T:T + HALO])
                    e.tensor_copy(out=kTp[:, :, 0:128], in_=kTp[:, :, T:T + 128])
                    return e.tensor_copy(out=Vp[:, 0, :, :], in_=Vp[:, 4, :, :])
                S.op("dve", carry, reads=s_u, writes=[s_uh, s_kT, s_V])
            if dbg_here:
                dbg_dump("d_mix", mixT.rearrange("p c t -> p (c t)"), s_mix)

            if upto < 3:
                continue
            accb = [0, 1, 4]
            na = 0
            for g in range(4):
                wb, wslot = w_acquire()
                for cc in range(4):
                    dc = 4 * g + cc
                    bank = accb[na % 3]
                    na += 1

                    def mm(e, wb=wb, cc=cc, bank=bank):
                        r = None
                        for kc in range(16):
                            r = e.matmul(PS[:, bank, :], lhsT=wb[:, kc, cc * 128:(cc + 1) * 128], rhs=mixT[:, kc, :], start=(kc == 0), stop=(kc == 15))
                        return r
                    S.op("pe", mm, reads=[wslot] + s_mix, writes=[s_B[bank]])
                    flush_stats()
                    resid_stats(dc, bank, bias_col=PC_BOUT + dc)
            flush_stats()
            ln_finish(1.0 / D)
            for dc in range(16):
                ln_normalize(R[:, dc, :], s_R[dc], dc)
                S.op("act", lambda e, dc=dc: e.activation(out=xT[:, dc, :], in_=R[:, dc, :], func=AF.Identity, bias=pcs(PC_B1 + dc),
                                                          scale=pcs(PC_G1 + dc)), reads=[s_c, s_R[dc]], writes=[s_xT[dc]])
                S.op("act", lambda e, dc=dc: e.activation(out=R[:, dc, :], in_=R[:, dc, :], func=AF.Identity, bias=ag[:, 16 + dc:17 + dc],
                                                          scale=ag[:, dc:dc + 1]), reads=[s_c], writes=[s_R[dc]])
            if dbg_here:
                dbg_dump("d_x1", R[:].rearrange("p c t -> p (c t)"), s_R)

            if upto < 4:
                continue
            accb = [0, 1, 4, 5]
            na = 0
            for g in range(22):
                wb, wslot = w_acquire()
                if g == 4 and ti + 1 < ntiles:
                    gen_tables(ti + 1)
                if g == 0:
                    kouter_group(wb, wslot, xT, s_xT, 16, accb)
                for cc in range(4):
                    f = 2 * g + cc // 2
                    bank = accb[na % 4]
                    na += 1

                    def mm(e, wb=wb, cc=cc, bank=bank):
                        r = None
                        for kc in range(16):
                            r = e.matmul(PS[:, bank, :], lhsT=wb[:, kc, cc * 128:(cc + 1) * 128], rhs=xT[:, kc, :], start=(kc == 0), stop=(kc == 15))
                        return r
                    if g > 0:
                        S.op("pe", mm, reads=[wslot] + s_xT, writes=[s_B[bank]])
                    sgi = f % 2
                    if cc % 2 == 0:
                        S.op("act", lambda e, bank=bank, sgi=sgi: e.activation(out=tmp[sgi][:], in_=PS[:, bank, :], func=AF.Silu),
                             reads=[s_B[bank]], writes=[s_tmp[sgi]])
                    else:
                        S.op("dve", lambda e, bank=bank, sgi=sgi, f=f: e.tensor_tensor(out=hT[:, f, :], in0=PS[:, bank, :], in1=tmp[sgi][:], op=ALU.mult),
                             reads=[s_B[bank], s_tmp[sgi]], writes=[s_h[f]])

            if upto < 5:
                continue
            if ti + 1 < ntiles:
                nseq = (ti + 1) // NTILE_SEQ
                ntok0 = nseq * SEQ + ((ti + 1) % NTILE_SEQ) * T
                for b in range(2):
                    load_x_block(ntok0, b, b)
            accb = [0, 1, 4, 5]
            for n in range(4):
                for kg in range(4):
                    wb, wslot = w_acquire()
                    for cc in range(4):
                        bank = accb[cc]

                        def mm(e, wb=wb, cc=cc, bank=bank, kg=kg):
                            r = None
                            for k in range(11):
                                r = e.matmul(PS[:, bank, :], lhsT=wb[:, k, cc * 128:(cc + 1) * 128], rhs=hT[:, kg * 11 + k, :],
                                             start=(kg == 0 and k == 0), stop=(kg == 3 and k == 10))
                            return r
                        S.op("pe", mm, reads=[wslot] + s_h[kg * 11:(kg + 1) * 11], writes=[s_B[bank]])
                    if kg == 0:
                        flush_stats()
                for cc in range(4):
                    resid_stats(4 * n + cc, accb[cc])
            flush_stats()
            if ti + 1 < ntiles:
                for b in range(2, 4):
                    load_x_block(ntok0, b, b)
            ln_finish(1.0 / D)

            def norm_group(g):
                for dc in range(4 * g, 4 * g + 4):
                    ln_normalize(R[:, dc, :], s_R[dc], dc)
                    S.op("act", lambda e, dc=dc: e.activation(out=R[:, dc, :], in_=R[:, dc, :], func=AF.Identity,
                                                              bias=pcs(PC_B2 + dc), scale=pcs(PC_G2 + dc)),
                         reads=[s_c], writes=[s_R[dc]])
            norm_group(0)
            for g in range(4):
                if g + 1 < 4:
                    norm_group(g + 1)
                for b in range(4):
                    k = 4 * g + b
                    bank = 6 + k % 2
                    pi = k % 8

                    def trf(e, g=g, b=b, bank=bank):
                        r = None
                        for i in range(4):
                            dc = 4 * g + i
                            r = e.transpose(out=PS[:, bank, i * 128:(i + 1) * 128], in_=R[:, dc, b * 128:(b + 1) * 128], identity=ident32[:])
                        return r
                    S.op("pe", trf, reads=s_R[4 * g:4 * g + 4] + [s_c], writes=[s_B[bank]])
                    if k % 2 == 0:
                        S.op("act", lambda e, pi=pi, bank=bank: e.activation(out=YY[:, pi, :], in_=PS[:, bank, :], func=AF.Identity),
                             reads=[s_B[bank]], writes=[s_Y[pi]])
                    else:
                        S.op("dve", lambda e, pi=pi, bank=bank: e.tensor_copy(out=YY[:, pi, :], in_=PS[:, bank, :]),
                             reads=[s_B[bank]], writes=[s_Y[pi]])
                    S.op("pool", lambda e, b=b, g=g, pi=pi, tok0=tok0: e.dma_start(
                        out=out_d[tok0 + b * 128: tok0 + (b + 1) * 128, g * 512:(g + 1) * 512], in_=YY[:, pi, :]),
                        reads=[s_Y[pi]], writes=[], dsem=d_yo[pi])

        fw = [(d, S.cnt[d]) for d in d_yo]
        for ds in dbg_sems:
            fw.append((ds, S.cnt[ds]))
        S.emit(fw)
    return nc


def _prep_shared(inp):
    f32 = np.float32
    qcols = []
    for c in range(8):
        qcols += list(range(c * 64, c * 64 + 64)) + list(range((8 + c) * 64, (8 + c) * 64 + 64))
    kcols = list(range(1024, 1152))
    vcols = list(range(1152, 1280))
    a0, g0 = 1280, 2304
    conv = []
    for c in range(8):
        conv += list(range(g0 + c * 128, g0 + (c + 1) * 128)) + list(range(a0 + c * 128, a0 + (c + 1) * 128))
    perm = np.array(qcols + conv + kcols + vcols)
    w_in = np.ascontiguousarray(np.asarray(inp["w_in"])[0][:, perm], dtype=f32)
    b_in = np.asarray(inp["b_in"])[0][perm].astype(f32)
    rows = np.array(qcols + list(range(1024, 2048)))
    w_out = np.ascontiguousarray(np.asarray(inp["w_out"])[0][rows, :], dtype=f32)
    wg = np.asarray(inp["w_gate"])[0].reshape(D, NF, 1, 128)
    wu = np.asarray(inp["w_up"])[0].reshape(D, NF, 1, 128)
    w_gu = np.ascontiguousarray(np.concatenate([wg, wu], axis=2).reshape(D, 2 * DFF), dtype=f32)
    w_down = np.ascontiguousarray(np.asarray(inp["w_down"])[0], dtype=f32)
    w_pw2 = np.ascontiguousarray(np.asarray(inp["w_pw2"])[0], dtype=f32)

    pcol = np.zeros((128, NPC), f32)
    pcol[:, PC_BIN:PC_BIN + 25] = b_in[:3200].reshape(25, 128).T
    col = lambda v, n: np.asarray(v)[0].astype(f32).reshape(n, 128).T
    pcol[:, PC_BDW:PC_BDW + 8] = col(inp["b_dw"], 8)
    pcol[:, PC_CLG:PC_CLG + 8] = col(inp["conv_ln_g"], 8)
    pcol[:, PC_CLB:PC_CLB + 8] = col(inp["conv_ln_b"], 8)
    pcol[:, PC_BPW:PC_BPW + 8] = col(inp["b_pw2"], 8)
    pcol[:, PC_BOUT:PC_BOUT + 16] = col(inp["b_out"], 16)
    pcol[:, PC_G1:PC_G1 + 16] = col(inp["ln1_g"], 16)
    pcol[:, PC_B1:PC_B1 + 16] = col(inp["ln1_b"], 16)
    pcol[:, PC_G2:PC_G2 + 16] = col(inp["ln2_g"], 16)
    pcol[:, PC_B2:PC_B2 + 16] = col(inp["ln2_b"], 16)
    half = 32
    inv_freq = (1.0 / (np.float32(10000.0) ** (np.arange(half, dtype=f32) * np.float32(2.0) / np.float32(64)))).astype(f32)
    p = np.arange(128)
    pcol[:, PC_INVF] = inv_freq[p % 32]
    pcol[:, PC_SIGN] = np.where((p % 64) < 32, -1.0, 1.0)
    wdw = np.asarray(inp["w_dw"])[0][:, 0, :].astype(f32)
    pcol[:, PC_WDW:] = wdw.reshape(KW, 8, 128).transpose(2, 1, 0).reshape(128, 8 * KW)

    prow = np.zeros((128, 144), f32)
    prow[:, 0:128] = b_in[3200:3328][None, :]
    sinks = np.asarray(inp["sinks"])[0].astype(f32)
    slot = np.array([sinks[(2 * (s // 4) + (s % 2)) + 8 * ((s % 4) // 2)] for s in range(16)], f32)
    prow[:, 128:144] = slot[None, :]

    cst = np.zeros((128, 512), f32)
    cst[:, 0:128] = np.eye(128, dtype=f32)
    m = np.arange(128)
    partner = np.where((m % 64) < 32, m + 32, m - 32)
    cst[partner, 128 + m] = 1.0
    qi = np.arange(128)[:, None]
    kj = np.arange(128)[None, :]
    cst[:, 256:384] = (kj > qi).astype(f32)
    cst[:, 384:512] = (kj <= qi).astype(f32)
    return dict(w_in=w_in, w_pw2=w_pw2, w_out=w_out, w_gu=w_gu, w_down=w_down, pcol=pcol, prow=prow, cst=cst)


def kernel(**inputs):
    shared = _prep_shared(inputs)
    x = np.asarray(inputs["x"], dtype=np.float32)
    pos = np.asarray(inputs["positions"]).astype(np.int32)
    nc = build_nc()
    in_maps = []
    for c in range(NCORES):
        m = dict(shared)
        m["x"] = np.ascontiguousarray(x[2 * c:2 * c + 2].reshape(2 * SEQ, D))
        m["pos"] = np.ascontiguousarray(pos[2 * c:2 * c + 2])
        in_maps.append(m)
    res = run_bass_kernel_spmd(nc, in_maps, core_ids=list(range(NCORES)))
    out = np.concatenate([np.asarray(r["out"]).reshape(2, SEQ, D) for r in res.results], axis=0)
    return out.astype(np.float32)
```
